# Optimizing a Trainium2 kernel written in Bass

```python
import jax, jax.numpy as jnp
from jax import lax
import numpy as np

D_MODEL = 1024
BATCH = 4
SEQ = 4096
DEPTH = 2
DEC_BATCH = 16
DEC_SEQ = 16
PAST_LEN = 2048

CHUNK = 64
N_MIXERS = 2
N_POOL_LAYERS = (DEPTH + 1) // 2
N_GLA_LAYERS = DEPTH // 2
POOL_WINDOWS = (2, 4, 8, 16)
N_POOL_GROUPS = len(POOL_WINDOWS)
POOL_GROUP = D_MODEL // N_POOL_GROUPS
POOL_HIST = max(POOL_WINDOWS) - 1
GLA_HEADS = 4
GLA_KEY_DIM = D_MODEL // 2
GLA_VAL_DIM = D_MODEL
GLA_DK = GLA_KEY_DIM // GLA_HEADS
GLA_DV = GLA_VAL_DIM // GLA_HEADS
GLA_GATE_RANK = 16
GLA_GATE_NORMALIZER = 16.0
GLA_IN = 2 * GLA_KEY_DIM + 2 * GLA_VAL_DIM + GLA_GATE_RANK
D_FF = -(-8 * D_MODEL // (3 * 256)) * 256
PLE_DIM = 256
EPS = 1e-6

kernel_name = "hybrid_pool_gla_streaming_step"


def rmsnorm(x, g):
    xf = x.astype(jnp.float32)
    y = xf * lax.rsqrt(jnp.mean(xf * xf, axis=-1, keepdims=True) + EPS)
    return (y * g.astype(jnp.float32)).astype(x.dtype)


def pool_mixer(xn, hist, start, w_pool, b_pool, scale):
    B, T, D = xn.shape
    full = jnp.concatenate([hist.astype(xn.dtype), xn], axis=1).astype(jnp.float32)
    csum = jnp.concatenate([jnp.zeros((B, 1, D), jnp.float32), jnp.cumsum(full, axis=1)], axis=1)
    pos = start + jnp.arange(T)
    diffs = []
    for gi, w in enumerate(POOL_WINDOWS):
        sl = slice(gi * POOL_GROUP, (gi + 1) * POOL_GROUP)
        hi = csum[:, POOL_HIST + 1:, sl]
        lo = csum[:, POOL_HIST + 1 - w:POOL_HIST + 1 - w + T, sl]
        cnt = jnp.minimum(w, pos + 1).astype(jnp.float32)
        diffs.append((hi - lo) / cnt[None, :, None] - full[:, POOL_HIST:, sl])
    d = jnp.stack(diffs, axis=2)
    y = jnp.einsum('btgc,gcd->btgd', d, w_pool.astype(jnp.float32)).reshape(B, T, D)
    y = (y + b_pool.astype(jnp.float32)) * scale.astype(jnp.float32)
    return y.astype(xn.dtype), full[:, -POOL_HIST:].astype(xn.dtype)


def gla_recurrence(q, k, v, log_a, s0):
    B, T = q.shape[:2]
    L = min(CHUNK, T)
    nb = T // L
    mask = jnp.tril(jnp.ones((L, L), dtype=bool))

    def to_blocks(t):
        return t.reshape(B, nb, L, *t.shape[2:]).swapaxes(0, 1)

    def step(S, xs):
        qb, kb, vb, gb = xs
        b = jnp.cumsum(gb, axis=1)
        q_dec = qb * jnp.exp(b)
        k_inv = kb * jnp.exp(-b)
        scores = jnp.where(mask, jnp.einsum('blhk,bmhk->bhlm', q_dec, k_inv), 0.0)
        o = jnp.einsum('bhlm,bmhv->blhv', scores, vb) + jnp.einsum('blhk,bhkv->blhv', q_dec, S)
        b_last = b[:, -1]
        k_end = kb * jnp.exp(b_last[:, None] - b)
        S_new = jnp.exp(b_last)[..., None] * S + jnp.einsum('blhk,blhv->bhkv', k_end, vb)
        return S_new, o

    S_fin, o = lax.scan(step, s0, (to_blocks(q), to_blocks(k), to_blocks(v), to_blocks(log_a)))
    return o.swapaxes(0, 1).reshape(B, T, GLA_HEADS, GLA_DV), S_fin


def gla_mixer(xn, s0, w_in, w_gate_up, b_gate, norm_w, w_out):
    B, T, _ = xn.shape
    proj = xn @ w_in
    q, k, v, g, gr = jnp.split(proj, [GLA_KEY_DIM, 2 * GLA_KEY_DIM, 2 * GLA_KEY_DIM + GLA_VAL_DIM,
                                      2 * GLA_KEY_DIM + 2 * GLA_VAL_DIM], axis=-1)
    log_a = jax.nn.log_sigmoid((gr @ w_gate_up + b_gate).astype(jnp.float32)) / GLA_GATE_NORMALIZER
    hk = (B, T, GLA_HEADS, GLA_DK)
    qh = q.astype(jnp.float32).reshape(hk) * (GLA_DK ** -0.5)
    kh = k.astype(jnp.float32).reshape(hk)
    vh = v.astype(jnp.float32).reshape(B, T, GLA_HEADS, GLA_DV)
    o, S = gla_recurrence(qh, kh, vh, log_a.reshape(hk), s0.astype(jnp.float32))
    o = o * lax.rsqrt(jnp.mean(o * o, axis=-1, keepdims=True) + EPS) * norm_w.astype(jnp.float32)
    o = o.reshape(B, T, GLA_VAL_DIM) * jax.nn.silu(g.astype(jnp.float32))
    return (o.astype(xn.dtype) @ w_out).astype(xn.dtype), S


def swiglu(xn, w_gate, w_up, w_down):
    return (jax.nn.silu(xn @ w_gate) * (xn @ w_up)) @ w_down


def trunk(x, p, pool_hist, gla_s0, start, norm_mix, norm_ffn, norm_ple, norm_final,
          w_pool, b_pool, pool_scale, w_gla_in, w_gla_gate_up, b_gla_gate, gla_norm, w_gla_out,
          w_ffn_gate, w_ffn_up, w_ffn_down, w_ple_proj, w_ple_gate):
    h = x
    pool_states, gla_states = [], []
    for i in range(DEPTH):
        j = i // N_MIXERS
        xn = rmsnorm(h, norm_mix[i])
        if i % N_MIXERS == 0:
            out, st = pool_mixer(xn, pool_hist[j], start, w_pool[j], b_pool[j], pool_scale[j])
            pool_states.append(st)
        else:
            out, st = gla_mixer(xn, gla_s0[j], w_gla_in[j], w_gla_gate_up[j], b_gla_gate[j],
                                gla_norm[j], w_gla_out[j])
            gla_states.append(st)
        h = h + out
        h = h + swiglu(rmsnorm(h, norm_ffn[i]), w_ffn_gate[i], w_ffn_up[i], w_ffn_down[i])
        gate = jax.nn.sigmoid(rmsnorm(h, norm_ple[i]) @ w_ple_gate[i])
        h = h + gate * (p[i].astype(h.dtype) @ w_ple_proj[i])
    return rmsnorm(h, norm_final), jnp.stack(pool_states), jnp.stack(gla_states)


def setup_inputs(seed: int = 0) -> dict:
    key = jax.random.key(seed)
    ks = jax.random.split(key, 32)
    f32 = jnp.float32

    def nrm(k, shape, scale=1.0):
        return jax.random.normal(k, shape, f32) * scale

    def gain(k, shape):
        return 1.0 + 0.05 * jax.random.normal(k, shape, f32)

    return {
        "x_prompt": nrm(ks[0], (BATCH, SEQ, D_MODEL)),
        "x_sample": nrm(ks[1], (DEC_BATCH, DEC_SEQ, D_MODEL)),
        "state_pool": nrm(ks[2], (N_POOL_LAYERS, DEC_BATCH, POOL_HIST, D_MODEL)),
        "state_gla": nrm(ks[3], (N_GLA_LAYERS, DEC_BATCH, GLA_HEADS, GLA_DK, GLA_DV), 0.5),
        "p_prompt": nrm(ks[4], (DEPTH, BATCH, SEQ, PLE_DIM)),
        "p_sample": nrm(ks[5], (DEPTH, DEC_BATCH, DEC_SEQ, PLE_DIM)),
        "norm_mix": gain(ks[6], (DEPTH, D_MODEL)),
        "norm_ffn": gain(ks[7], (DEPTH, D_MODEL)),
        "norm_ple": gain(ks[8], (DEPTH, D_MODEL)),
        "norm_final": gain(ks[9], (D_MODEL,)),
        "w_pool": nrm(ks[10], (N_POOL_LAYERS, N_POOL_GROUPS, POOL_GROUP, POOL_GROUP), POOL_GROUP ** -0.5),
        "b_pool": nrm(ks[11], (N_POOL_LAYERS, D_MODEL), 0.02),
        "pool_scale": 0.5 + 0.05 * jax.random.normal(ks[12], (N_POOL_LAYERS, D_MODEL), f32),
        "w_gla_in": nrm(ks[13], (N_GLA_LAYERS, D_MODEL, GLA_IN), D_MODEL ** -0.5),
        "w_gla_gate_up": nrm(ks[14], (N_GLA_LAYERS, GLA_GATE_RANK, GLA_KEY_DIM), GLA_GATE_RANK ** -0.5),
        "b_gla_gate": nrm(ks[15], (N_GLA_LAYERS, GLA_KEY_DIM), 0.1),
        "gla_norm": gain(ks[16], (N_GLA_LAYERS, GLA_DV)),
        "w_gla_out": nrm(ks[17], (N_GLA_LAYERS, GLA_VAL_DIM, D_MODEL), GLA_VAL_DIM ** -0.5),
        "w_ffn_gate": nrm(ks[18], (DEPTH, D_MODEL, D_FF), D_MODEL ** -0.5),
        "w_ffn_up": nrm(ks[19], (DEPTH, D_MODEL, D_FF), D_MODEL ** -0.5),
        "w_ffn_down": nrm(ks[20], (DEPTH, D_FF, D_MODEL), D_FF ** -0.5),
        "w_ple_proj": nrm(ks[21], (DEPTH, PLE_DIM, D_MODEL), PLE_DIM ** -0.5),
        "w_ple_gate": nrm(ks[22], (DEPTH, D_MODEL, D_MODEL), D_MODEL ** -0.5),
    }


def reference(x_prompt, x_sample, state_pool, state_gla, p_prompt, p_sample,
              norm_mix, norm_ffn, norm_ple, norm_final, w_pool, b_pool, pool_scale,
              w_gla_in, w_gla_gate_up, b_gla_gate, gla_norm, w_gla_out,
              w_ffn_gate, w_ffn_up, w_ffn_down, w_ple_proj, w_ple_gate):
    weights = (norm_mix, norm_ffn, norm_ple, norm_final, w_pool, b_pool, pool_scale,
               w_gla_in, w_gla_gate_up, b_gla_gate, gla_norm, w_gla_out,
               w_ffn_gate, w_ffn_up, w_ffn_down, w_ple_proj, w_ple_gate)
    b = x_prompt.shape[0]
    pool_hist0 = jnp.zeros((N_POOL_LAYERS, b, POOL_HIST, D_MODEL), x_prompt.dtype)
    gla_s00 = jnp.zeros((N_GLA_LAYERS, b, GLA_HEADS, GLA_DK, GLA_DV), jnp.float32)
    y_prompt, pool_state_prompt, gla_state_prompt = trunk(
        x_prompt, p_prompt, pool_hist0, gla_s00, 0, *weights)
    y_sample, pool_state_sample, gla_state_sample = trunk(
        x_sample, p_sample, state_pool, state_gla, PAST_LEN, *weights)
    return (y_prompt, y_sample, pool_state_prompt, pool_state_sample, gla_state_prompt, gla_state_sample)
```

```python
import contextlib
import types
import numpy as np
import concourse.bass as bass
import concourse.mybir as mybir
from concourse.bass_utils import run_bass_kernel_spmd

F32 = mybir.dt.float32
BF16 = mybir.dt.bfloat16
AF = mybir.ActivationFunctionType
ALU = mybir.AluOpType

D = 1024
NCH = 8
SEG = 2048
NT = 2112
DFF = 2816
NF = 22
GIN = 3088
EPS = 1e-6
WINS = (2, 4, 8, 16)
SLABS = (4, 4, 4, 4, 3, 3)
GT = 256
ATTACH_WAIT = True
PERSIST = ("h:", "pb", "cols", "ident", "tri", "um", "cm", "wgu", "bgate", "ones", "m16", "kc", "useprev", "gn_col", "Sst", "Sbf", "ssq", "l0n_ln", "l0rs", "fin_ln", "finrs")
SAME_ENGINE_SYNC = True


class Sched:
    ENGS = ("pe", "act", "dve", "pool", "sp")

    def __init__(self, nc):
        self.nc = nc
        self.ops = {e: [] for e in self.ENGS}
        self.res = {}
        self.last_dma = {}
        self.seq = 0
        self.known = set()
        self.check = False

    @staticmethod
    def _freeze(fn):
        if fn is None or fn.__closure__ is None:
            return fn
        cells = []
        for c in fn.__closure__:
            try:
                v = c.cell_contents
            except ValueError:
                cells.append(c)
                continue
            if isinstance(v, types.FunctionType):
                v = Sched._freeze(v)
            cells.append(types.CellType(v))
        return types.FunctionType(fn.__code__, fn.__globals__, fn.__name__, fn.__defaults__, tuple(cells))

    def op(self, eng, fn, reads=(), writes=(), dma=None, ndma=1):
        fn = self._freeze(fn)
        if self.check:
            for nm in list(reads) + list(writes):
                assert nm in self.known or nm.startswith(PERSIST), ("unregistered resource", nm)
        rec = {"eng": eng, "fn": fn, "deps": [], "inc": False, "dma": dma, "ndma": ndma, "tok": None, "seq": self.seq}
        self.seq += 1
        deps = []
        for r in reads:
            st = self.res.setdefault(r, {"w": None, "r": []})
            if st["w"] is not None:
                deps.append(st["w"])
        for w in writes:
            st = self.res.setdefault(w, {"w": None, "r": []})
            if st["w"] is not None:
                deps.append(st["w"])
            deps.extend(st["r"])
        for r in reads:
            self.res[r]["r"].append(rec)
        for w in writes:
            st = self.res[w]
            st["w"] = rec
            st["r"] = []
        latest = {}
        keep = []
        for d in deps:
            if d is rec:
                continue
            if d["dma"] is not None:
                keep.append(d)
                continue
            if d["eng"] == eng and (not SAME_ENGINE_SYNC or eng in ("pe", "sp")):
                continue
            cur = latest.get(d["eng"])
            if cur is None or d["seq"] > cur["seq"]:
                latest[d["eng"]] = d
        seen = set()
        for d in keep + list(latest.values()):
            if id(d) in seen:
                continue
            seen.add(id(d))
            rec["deps"].append(d)
            d["inc"] = True
        if dma is not None:
            rec["inc"] = True
            self.last_dma[dma] = rec
        self.ops[eng].append(rec)
        return rec

    def fence(self):
        lasts = []
        for e in self.ENGS:
            for rec in reversed(self.ops[e]):
                if rec["dma"] is None and rec["fn"] is not None:
                    lasts.append(rec)
                    break
        lasts.extend(self.last_dma.values())
        for e in self.ENGS:
            rec = {"eng": e, "fn": None, "deps": [], "inc": False, "dma": None, "ndma": 0, "tok": None, "seq": self.seq}
            self.seq += 1
            for d in lasts:
                if d["eng"] == e and d["dma"] is None and (not SAME_ENGINE_SYNC or e in ("pe", "sp")):
                    continue
                rec["deps"].append(d)
                d["inc"] = True
            self.ops[e].append(rec)
        self.res = {}

    def emit(self):
        nc = self.nc
        counts = {}
        for e in self.ENGS:
            for rec in self.ops[e]:
                if not rec["inc"]:
                    continue
                key = ("dma", rec["dma"]) if rec["dma"] is not None else ("eng", e)
                step = 16 * rec["ndma"] if rec["dma"] is not None else 1
                counts[key] = counts.get(key, 0) + step
                rec["tok"] = (key, counts[key])
        with contextlib.ExitStack() as st:
            sems = {}
            for i, k in enumerate(counts.keys()):
                sems[k] = st.enter_context(nc.semaphore("s%d" % i))
            block = st.enter_context(nc.Block())
            engmap = {"pe": block.tensor, "act": block.scalar, "dve": block.vector,
                      "pool": block.gpsimd, "sp": block.sync}

            def make(e):
                def body(eng):
                    waited = {}
                    for rec in self.ops[e]:
                        need = []
                        for d in rec["deps"]:
                            k, v = d["tok"]
                            if waited.get(k, 0) >= v:
                                continue
                            waited[k] = v
                            need.append((k, v))
                        attach = None
                        if ATTACH_WAIT and need and rec["fn"] is not None and rec["dma"] is None:
                            attach = need.pop()
                        for k, v in need:
                            eng.wait_ge(sems[k], v)
                        if rec["fn"] is None:
                            continue
                        ins = rec["fn"](eng)
                        if attach is not None:
                            ins._wait_ge(sems[attach[0]], attach[1])
                        if rec["inc"]:
                            if rec["dma"] is not None:
                                lst = ins if isinstance(ins, (list, tuple)) else [ins]
                                assert len(lst) == rec["ndma"], (len(lst), rec["ndma"])
                                for x in lst:
                                    x.then_inc(sems[rec["tok"][0]], 16)
                            else:
                                ins.then_inc(sems[rec["tok"][0]], 1)
                    if e == "sp":
                        for k, v in counts.items():
                            if k[0] == "dma" and waited.get(k, 0) < v:
                                eng.wait_ge(sems[k], v)
                return body

            for e in self.ENGS:
                engmap[e](make(e))
        return {e: len(self.ops[e]) for e in self.ENGS}


def _tiles(n):
    out = []
    c = 0
    while c < min(n, SEG):
        out.append((c, 512))
        c += 512
    if n > SEG:
        out.append((SEG, n - SEG))
    return out


def build_program(stop_stage=None):
    nc = bass.Bass("TRN2", target_bir_lowering=False)

    def din(name, shape):
        return nc.dram_tensor(name, list(shape), F32, kind="ExternalInput").ap()

    def dout(name, shape):
        return nc.dram_tensor(name, list(shape), F32, kind="ExternalOutput").ap()

    x_main = din("x_main", [SEG, D]); x_mprev = din("x_mprev", [128, D])
    x_warm = din("x_warm", [SEG, D]); x_wprev = din("x_wprev", [128, D])
    xs_in = din("xs", [32, D]); hist_in = din("hist", [30, D]); sgla_in = din("sgla", [2, 4, 128, 256])
    p_main = din("p_main", [2, SEG, 256]); p_warm = din("p_warm", [SEG, 256]); p_smp = din("p_smp", [2, 32, 256])
    useprev_in = din("useprev", [128, 1])
    band0m_in = din("band0m", [4, 128, 128])
    c_ident = din("c_ident", [128, 128]); c_sel = din("c_sel", [64, 64])
    c_band = din("c_band", [4, 128, 128]); c_halo = din("c_halo", [4, 128, 128]); c_band0w = din("c_band0w", [4, 128, 128])
    c_bands = din("c_bands", [4, 64, 64])
    c_tri = din("c_tri", [128, 128]); c_u = din("c_u", [128, 128]); c_cm = din("c_cm", [128, 128])
    c_tris = din("c_tris", [64, 64]); c_us = din("c_us", [64, 64]); c_cms = din("c_cms", [64, 64])
    norm_mix = din("norm_mix", [2, D]); norm_ffn = din("norm_ffn", [2, D]); norm_ple = din("norm_ple", [2, D])
    norm_final = din("norm_final", [D])
    w_pool = din("w_pool", [4, 256, 256]); b_pool = din("b_pool", [D]); pool_scale = din("pool_scale", [D])
    w_gin = din("w_gla_in", [D, GIN]); w_ggu = din("w_gla_gate_up", [16, 512]); b_gg = din("b_gla_gate", [512])
    gla_norm = din("gla_norm", [256]); w_gout = din("w_gla_out", [D, D])
    w_fg = din("w_ffn_gate", [2, D, DFF]); w_fu = din("w_ffn_up", [2, D, DFF]); w_fd = din("w_ffn_down", [2, DFF, D])
    w_pp = din("w_ple_proj", [2, 256, D]); w_pg = din("w_ple_gate", [2, D, D])
    y_main = dout("y_main", [SEG, D]); y_smp = dout("y_smp", [32, D])
    ps_main = dout("ps_main", [15, D]); ps_smp = dout("ps_smp", [2, 15, D])
    gs_main = dout("gs_main", [4, 128, 256]); gs_smp = dout("gs_smp", [2, 4, 128, 256])

    S = Sched(nc)
    S.check = True
    st = contextlib.ExitStack()
    with st:
        def sb(name, shape, dt):
            return st.enter_context(nc.sbuf_tensor(name, list(shape), dt))

        pbank = [st.enter_context(nc.psum_tensor("pb%d" % i, [128, 512], F32)) for i in range(8)]
        rr = {"i": 0, "n": 8}

        def bank():
            i = rr["i"] % rr["n"]
            rr["i"] += 1
            return pbank[i], "pb%d" % i

        h = sb("h", [128, NCH, NT], F32)
        Sst = sb("Sst", [128, 3, 4, 256], F32)
        Sbf = sb("Sbf", [128, 3, 4, 256], BF16)
        ident = sb("ident", [128, 128], F32)
        ones_b = sb("ones_b", [128, 128], BF16); ones_row = sb("ones_row", [1, 128], BF16)
        tri = sb("tri", [128, 128], F32); um = sb("um", [128, 128], F32); cm = sb("cm", [128, 128], F32)
        tris = sb("tris", [64, 64], F32); ums = sb("ums", [64, 64], F32); cms = sb("cms", [64, 64], F32)
        cols = sb("cols", [128, 10, NCH], F32)
        gn_col = sb("gn_col", [128, 2], F32)
        m16 = sb("m16", [128, 1], F32)
        useprev = sb("useprev_sb", [128, 1], F32)
        wgu = sb("wgu", [16, 512], BF16); bgate = sb("bgate", [1, 512], BF16)
        ssq_t = sb("ssq_t", [128, 2], F32); ssq2_t = sb("ssq2_t", [128, 1], F32)
        ln_t = sb("ln_t", [128, 1], F32); rs_t = sb("rs_t", [128, 1], F32)
        kc = sb("kc", [128, 4], F32)

        arena_words = (nc.sbuf_bytes_remaining - 512) // 4
        arena = sb("arena", [128, arena_words], F32)
        ar = {"off": 0}

        areg = []
        ghosts = []

        def areset():
            ar["off"] = 0

        def aalloc(shape, dt, names, parts=128):
            n = int(np.prod(shape))
            words = n if dt == F32 else (n + 1) // 2
            o = ar["off"]
            ar["off"] += words
            assert ar["off"] <= arena_words, ("arena overflow", ar["off"], arena_words)
            lo, hi = o, o + words
            inherited = []
            keepl = []
            for (l2, h2, nm2) in areg:
                if l2 < hi and lo < h2:
                    ops_ = []
                    for nm in nm2:
                        stt = S.res.pop(nm, None)
                        if stt is not None:
                            if stt["w"] is not None:
                                ops_.append(stt["w"])
                            ops_.extend(stt["r"])
                    inherited.extend(ops_)
                    if not (lo <= l2 and h2 <= hi) and ops_:
                        ghosts.append((l2, h2, ops_))
                else:
                    keepl.append((l2, h2, nm2))
            keepg = []
            for (l2, h2, ops_) in ghosts:
                if l2 < hi and lo < h2:
                    inherited.extend(ops_)
                    if lo <= l2 and h2 <= hi:
                        continue
                keepg.append((l2, h2, ops_))
            ghosts[:] = keepg
            nset = set(names)
            areg[:] = [(l2, h2, tuple(x for x in nm2 if x not in nset)) for (l2, h2, nm2) in keepl]
            areg.append((lo, hi, tuple(names)))
            for nm in names:
                S.res[nm] = {"w": None, "r": list(inherited)}
                S.known.add(nm)
            v = arena[0:parts, o:o + words]
            if dt != F32:
                v = v.bitcast(BF16)
                if n % 2:
                    v = v[:, 0:n]
            if len(shape) == 2:
                return v.rearrange("p (a b) -> p a b", a=shape[0])
            if len(shape) == 3:
                return v.rearrange("p (a b c) -> p a b c", a=shape[0], b=shape[1])
            return v

        def hres(c0, W):
            return ["h:%d" % j for j in range(c0 // 256, (c0 + W - 1) // 256 + 1)]

        def ld(dst, src, key, eng="sp", writes=None):
            S.op(eng, lambda e: e.dma_start(out=dst, in_=src), writes=writes or [key], dma=key)

        ld(ident[:], c_ident[:, :], "ident")
        ld(tri[:], c_tri[:, :], "tri"); ld(um[:], c_u[:, :], "um"); ld(cm[:], c_cm[:, :], "cm")
        ld(tris[:], c_tris[:, :], "tris"); ld(ums[:], c_us[:, :], "ums"); ld(cms[:], c_cms[:, :], "cms")
        ld(useprev[:], useprev_in[:, :], "useprev")
        ld(wgu[:], w_ggu[:, :], "wgu", eng="pool")
        ld(bgate[:], b_gg.rearrange("(o n) -> o n", o=1), "bgate", eng="pool")
        with nc.allow_non_contiguous_dma(reason="tiny per-feature vectors to per-partition columns"):
            vecs = [(0, norm_ffn[0]), (1, norm_ffn[1]), (2, norm_ple[0]), (3, norm_ple[1]), (4, norm_mix[1]),
                    (5, b_pool), (6, pool_scale)]
            for i, v in vecs:
                S.op("sp", (lambda i, v: lambda e: e.dma_start(out=cols[:, i, :], in_=v.rearrange("(c p) -> p c", p=128), allow_slow_non_contiguous=True))(i, v),
                     writes=["cols%d" % i], dma="cols%d" % i)
            S.op("sp", lambda e: e.dma_start(out=gn_col[:], in_=gla_norm.rearrange("(c p) -> p c", p=128), allow_slow_non_contiguous=True),
                 writes=["gn_col"], dma="gn_col")
        S.op("dve", lambda e: e.memset(ones_b[:], 1.0), writes=["ones_b"])
        S.op("dve", lambda e: e.memset(ones_row[:], 1.0), writes=["ones_row"])
        S.op("dve", lambda e: e.memset(m16[:], -1.0 / 16.0), writes=["m16"])
        S.op("dve", lambda e: e.memset(kc[:, 0:1], 1.0 / D), writes=["kc"])
        S.op("dve", lambda e: e.memset(kc[:, 1:2], -0.5), writes=["kc"])
        S.op("dve", lambda e: e.memset(kc[:, 2:3], 1.0 / 256.0), writes=["kc"])
        S.op("dve", lambda e: e.tensor_tensor(out=cols[:, 7, :], in0=cols[:, 5, :], in1=cols[:, 6, :], op=ALU.mult),
             reads=["cols5", "cols6"], writes=["cols7"])
        S.fence()

        def rstd_from_ss(ss_ap, out_ap, n, tag, reads, writes, tmp_ap):
            rows = ss_ap.shape[0]
            ki = 0 if n == float(D) else 2
            S.op("act", lambda e: e.activation(out=tmp_ap, in_=ss_ap, func=AF.Ln, scale=kc[0:rows, ki:ki + 1], bias=EPS),
                 reads=reads, writes=[tag + "_ln"])
            S.op("act", lambda e: e.activation(out=out_ap, in_=tmp_ap, func=AF.Exp, scale=kc[0:rows, 1:2]),
                 reads=[tag + "_ln"], writes=writes)

        def norm_fm(xn, gidx, c0, W, scr, tag, xname=None):
            sq, lnv, rs = scr
            S.op("act", lambda e: e.activation(out=sq[:, :, 0:W], in_=h[:, :, c0:c0 + W], func=AF.Square),
                 reads=hres(c0, W), writes=[tag + "sq"])
            pb, pn = bank()
            for c in range(NCH):
                S.op("pe", (lambda c: lambda e: e.matmul(pb[:, 0:W], lhsT=ones_b[:], rhs=sq[:, c, 0:W],
                                                          start=(c == 0), stop=(c == NCH - 1)))(c),
                     reads=[tag + "sq"], writes=[pn])
            rstd_from_ss(pb[:, 0:W], rs[:, 0:W], float(D), tag, [pn], [tag + "rs"], lnv[:, 0:W])
            for c in range(NCH):
                S.op("dve", (lambda c: lambda e: e.scalar_tensor_tensor(
                    out=xn[:, c, 0:W], in0=h[:, c, c0:c0 + W], scalar=cols[:, gidx, c:c + 1], in1=rs[:, 0:W],
                    op0=ALU.mult, op1=ALU.mult))(c),
                    reads=hres(c0, W) + [tag + "rs"], writes=[xname or (tag + "xn")])

        def l0_mixer(x_tok, x_prev, b0_dram, ntok, with_sample, emit_state):
            areset()
            xt = aalloc([2, D], F32, ["xt0", "xt1"])
            xnb = aalloc([2, D], BF16, ["xnb0", "xnb1"])
            junk = aalloc([D], BF16, ["junk"])
            gbc = aalloc([D], F32, ["gbc"])
            xnf = aalloc([D], F32, ["xnf"])
            dT = aalloc([NCH, ntok], BF16, ["dT:%d" % i for i in range(5)])
            wp = aalloc([4, 2, 256], BF16, ["wp"])
            band = aalloc([4, 128], BF16, ["band"]); halo = aalloc([4, 128], BF16, ["halo"]); b0 = aalloc([4, 128], BF16, ["b0"])
            bands = aalloc([4, 64], BF16, ["bands"], parts=64); sel = aalloc([64], F32, ["sel"], parts=64)
            ld(band, c_band.rearrange("w s t -> s w t"), "band", eng="pool")
            ld(halo, c_halo.rearrange("w s t -> s w t"), "halo", eng="pool")
            ld(b0, b0_dram.rearrange("w s t -> s w t"), "b0", eng="pool")
            ld(bands, c_bands.rearrange("w s t -> s w t"), "bands", eng="pool")
            ld(sel, c_sel[:, :], "sel")
            ld(gbc, norm_mix[0].partition_broadcast(128), "gbc")
            S.op("pool", lambda e: e.dma_start(out=wp, in_=w_pool.rearrange("g (k p) n -> p g k n", p=128)),
                 writes=["wp"], dma="wp")
            ntile = SEG // 128

            ntc = {"i": 0}

            def norm_tile(src_ap, slot, xslot, rows, last):
                ci = ntc["i"] % 2
                ntc["i"] += 1
                sn = "ssq%d" % ci
                S.op("sp", lambda e: e.dma_start(out=xt[0:rows, xslot, :], in_=src_ap), writes=["xt%d" % xslot], dma="xt%d" % xslot)
                S.op("pool", lambda e: e.memset(ssq_t[:, ci:ci + 1], 0.0), writes=[sn])
                S.op("act", lambda e: e.activation(out=junk[0:rows, :], in_=xt[0:rows, xslot, :], func=AF.Square,
                                                   accum_out=ssq_t[0:rows, ci:ci + 1]),
                     reads=["xt%d" % xslot], writes=[sn, "junk"])
                rstd_from_ss(ssq_t[0:rows, ci:ci + 1], rs_t[0:rows, 0:1], float(D), "l0n", [sn], ["l0rs"], ln_t[0:rows, 0:1])
                S.op("dve", lambda e: e.scalar_tensor_tensor(out=xnb[0:rows, slot, :], in0=xt[0:rows, xslot, :],
                                                             scalar=rs_t[0:rows, 0:1], in1=gbc[0:rows, :], op0=ALU.mult, op1=ALU.mult),
                     reads=["xt%d" % xslot, "l0rs", "gbc"], writes=["xnb%d" % slot])
                if last:
                    S.op("dve", lambda e: e.scalar_tensor_tensor(out=xnf[0:rows, :], in0=xt[0:rows, xslot, :],
                                                                 scalar=rs_t[0:rows, 0:1], in1=gbc[0:rows, :], op0=ALU.mult, op1=ALU.mult),
                         reads=["xt%d" % xslot, "l0rs", "gbc"], writes=["xnf"])

            norm_tile(x_prev[:, :], 1, 1, 128, False)
            prev_slot = 1
            for i in range(ntile):
                slot = i % 2
                xslot = i % 2
                last = (i == ntile - 1) and emit_state
                norm_tile(x_tok[i * 128:(i + 1) * 128, :], slot, xslot, 128, last)
                tt = (i * 128) // 512
                for half in range(2):
                    pb, pn = bank()
                    for q in range(4):
                        c = half * 4 + q
                        S.op("pe", (lambda c, q, pb: lambda e: e.matmul(pb[:, q * 128:(q + 1) * 128], lhsT=xt[:, xslot, c * 128:(c + 1) * 128],
                                                                        rhs=ident[:], start=True, stop=True))(c, q, pb),
                             reads=["xt%d" % xslot, "ident"], writes=[pn])
                    for q in range(4):
                        c = half * 4 + q
                        S.op("act", (lambda c, q, pb: lambda e: e.activation(out=h[:, c, i * 128:(i + 1) * 128], in_=pb[:, q * 128:(q + 1) * 128],
                                                                             func=AF.Identity, bias=cols[:, 7, c:c + 1]))(c, q, pb),
                             reads=[pn, "cols7"], writes=hres(i * 128, 128))
                bsel = b0 if i == 0 else band
                for half in range(2):
                    pb, pn = bank()
                    for q in range(4):
                        c = half * 4 + q
                        w = c // 2
                        S.op("pe", (lambda c, q, w, pb, bsel: lambda e: e.matmul(pb[:, q * 128:(q + 1) * 128], lhsT=xnb[:, slot, c * 128:(c + 1) * 128],
                                                                                 rhs=bsel[:, w, :], start=True, stop=False))(c, q, w, pb, bsel),
                             reads=["xnb%d" % slot, "band", "b0"], writes=[pn])
                        S.op("pe", (lambda c, q, w, pb, ps_: lambda e: e.matmul(pb[:, q * 128:(q + 1) * 128], lhsT=xnb[64:128, ps_, c * 128:(c + 1) * 128],
                                                                                rhs=halo[64:128, w, :], start=False, stop=True))(c, q, w, pb, prev_slot),
                             reads=["xnb%d" % prev_slot, "halo"], writes=[pn])
                    S.op("dve", (lambda half, pb: lambda e: e.tensor_copy(out=dT[:, half * 4:(half + 1) * 4, i * 128:(i + 1) * 128],
                                                                          in_=pb[:].rearrange("p (a b) -> p a b", a=4)))(half, pb),
                         reads=[pn], writes=["dT:%d" % tt])
                if last:
                    S.op("sp", lambda e: e.dma_start(out=ps_main[:, :], in_=xnf[113:128, :]), reads=["xnf"], dma="o_psm")
                prev_slot = slot
            if with_sample:
                A = aalloc([D], F32, ["A"], parts=64)
                Hh = aalloc([D], F32, ["Hh"], parts=64)
                xsf = aalloc([D], F32, ["xsf"], parts=64)
                xsb = aalloc([D], BF16, ["xsb"], parts=64)
                S.op("dve", lambda e: e.memset(A, 0.0), writes=["A"])
                S.op("dve", lambda e: e.memset(Hh, 0.0), writes=["Hh"])
                for j in range(2):
                    S.op("sp", (lambda j: lambda e: e.dma_start(out=A[32 * j + 15:32 * j + 31, :], in_=xs_in[16 * j:16 * j + 16, :]))(j),
                         reads=[], writes=["A"], dma="A%d" % j)
                    S.op("sp", (lambda j: lambda e: e.dma_start(out=Hh[32 * j:32 * j + 15, :], in_=hist_in[15 * j:15 * j + 15, :]))(j),
                         reads=[], writes=["Hh"], dma="H%d" % j)
                S.op("dve", lambda e: e.memset(ssq_t[:, 0:1], 0.0), writes=["ssq0"])
                S.op("act", lambda e: e.activation(out=junk[0:64, :], in_=A, func=AF.Square, accum_out=ssq_t[0:64, 0:1]),
                     reads=["A"], writes=["ssq0", "junk"])
                rstd_from_ss(ssq_t[0:64, 0:1], rs_t[0:64, 0:1], float(D), "l0n", ["ssq0"], ["l0rs"], ln_t[0:64, 0:1])
                S.op("dve", lambda e: e.scalar_tensor_tensor(out=xsf, in0=A, scalar=rs_t[0:64, 0:1], in1=gbc[0:64, :],
                                                             op0=ALU.mult, op1=ALU.mult), reads=["A", "l0rs", "gbc"], writes=["xsf"])
                S.op("dve", lambda e: e.tensor_tensor(out=xsf, in0=xsf, in1=Hh, op=ALU.add), reads=["xsf", "Hh"], writes=["xsf"])
                S.op("dve", lambda e: e.tensor_copy(out=xsb, in_=xsf), reads=["xsf"], writes=["xsb"])
                for j in range(2):
                    S.op("sp", (lambda j: lambda e: e.dma_start(out=ps_smp[j, :, :], in_=xsf[32 * j + 16:32 * j + 31, :]))(j),
                         reads=["xsf"], dma="o_pss%d" % j)
                for half in range(2):
                    pb, pn = bank()
                    for q in range(4):
                        c = half * 4 + q
                        S.op("pe", (lambda c, q, pb: lambda e: e.matmul(pb[:, q * 64:(q + 1) * 64], lhsT=A[:, c * 128:(c + 1) * 128],
                                                                        rhs=sel, start=True, stop=True))(c, q, pb),
                             reads=["A", "sel"], writes=[pn])
                    for q in range(4):
                        c = half * 4 + q
                        S.op("act", (lambda c, q, pb: lambda e: e.activation(out=h[:, c, SEG:NT], in_=pb[:, q * 64:(q + 1) * 64],
                                                                             func=AF.Identity, bias=cols[:, 7, c:c + 1]))(c, q, pb),
                             reads=[pn, "cols7"], writes=hres(SEG, 64))
                for half in range(2):
                    pb, pn = bank()
                    for q in range(4):
                        c = half * 4 + q
                        w = c // 2
                        S.op("pe", (lambda c, q, w, pb: lambda e: e.matmul(pb[:, q * 64:(q + 1) * 64], lhsT=xsb[:, c * 128:(c + 1) * 128],
                                                                           rhs=bands[:, w, :], start=True, stop=True))(c, q, w, pb),
                             reads=["xsb", "bands"], writes=[pn])
                    S.op("dve", (lambda half, pb: lambda e: e.tensor_copy(out=dT[:, half * 4:(half + 1) * 4, SEG:NT],
                                                                          in_=pb[:, 0:256].rearrange("p (a b) -> p a b", a=4)))(half, pb),
                         reads=[pn], writes=["dT:4"])
            for g in range(4):
                for mc in range(2):
                    c = 2 * g + mc
                    for (c0, W) in _tiles(ntok):
                        tt = c0 // 512
                        pb, pn = bank()
                        for kc in range(2):
                            S.op("pe", (lambda g, mc, kc, pb, c0, W: lambda e: e.matmul(
                                pb[:, 0:W], lhsT=wp[:, g, kc, mc * 128:(mc + 1) * 128], rhs=dT[:, 2 * g + kc, c0:c0 + W],
                                start=(kc == 0), stop=(kc == 1)))(g, mc, kc, pb, c0, W),
                                reads=["wp", "dT:%d" % tt], writes=[pn])
                        S.op("dve", (lambda c, pb, c0, W: lambda e: e.scalar_tensor_tensor(
                            out=h[:, c, c0:c0 + W], in0=pb[:, 0:W], scalar=cols[:, 6, c:c + 1], in1=h[:, c, c0:c0 + W],
                            op0=ALU.mult, op1=ALU.add))(c, pb, c0, W),
                            reads=[pn] + hres(c0, W), writes=hres(c0, W))

        def ffn(layer, ntok, hook=None):
            areset()
            xn = aalloc([NCH, ntok], BF16, ["fnxn"])
            FS = max(SLABS)
            wg = [aalloc([NCH, FS * 128], BF16, ["wg%d" % i]) for i in range(2)]
            wu = [aalloc([NCH, FS * 128], BF16, ["wu%d" % i]) for i in range(2)]
            wd = [aalloc([FS, D], BF16, ["wd%d" % i]) for i in range(2)]
            mark = ar["off"]
            sq = aalloc([NCH, 512], BF16, ["fnsq"]); lnv = aalloc([512], F32, ["fn_ln"]); rs = aalloc([512], F32, ["fnrs"])

            def load_slab(s, f0):
                fs = SLABS[s]
                b = s % 2
                S.op("pool", lambda e: e.dma_start(out=wg[b][:, :, 0:fs * 128],
                                                   in_=w_fg[layer].rearrange("(c p) f -> p c f", p=128)[:, :, f0 * 128:(f0 + fs) * 128]),
                     writes=["wg%d" % b], dma="wg%d" % b)
                S.op("pool", lambda e: e.dma_start(out=wu[b][:, :, 0:fs * 128],
                                                   in_=w_fu[layer].rearrange("(c p) f -> p c f", p=128)[:, :, f0 * 128:(f0 + fs) * 128]),
                     writes=["wu%d" % b], dma="wu%d" % b)
                S.op("pool", lambda e: e.dma_start(out=wd[b][:, 0:fs, :],
                                                   in_=w_fd[layer, f0 * 128:(f0 + fs) * 128, :].rearrange("(f p) n -> p f n", p=128)),
                     writes=["wd%d" % b], dma="wd%d" % b)

            f0s = [sum(SLABS[:s]) for s in range(len(SLABS))]
            load_slab(0, f0s[0])
            load_slab(1, f0s[1])
            hooked = hook() if hook is not None else None
            for (c0, W) in _tiles(ntok):
                norm_fm(xn[:, :, c0:c0 + W], layer, c0, W, (sq, lnv, rs), "fn")
            ar["off"] = mark
            act = [aalloc([FS, 512], BF16, ["act%d" % i]) for i in range(2)]
            sg = [aalloc([512], F32, ["sg%d" % i]) for i in range(2)]
            it = 0
            for s, fs in enumerate(SLABS):
                b = s % 2
                for (c0, W) in _tiles(ntok):
                    tt = c0 // 512
                    ab = it % 2
                    it += 1
                    for f in range(fs):
                        pg, png = bank()
                        pu, pnu = bank()
                        for k in range(NCH):
                            S.op("pe", (lambda f, k, pg: lambda e: e.matmul(pg[:, 0:W], lhsT=wg[b][:, k, f * 128:(f + 1) * 128],
                                                                            rhs=xn[:, k, c0:c0 + W], start=(k == 0), stop=(k == NCH - 1)))(f, k, pg),
                                 reads=["wg%d" % b, "fnxn"], writes=[png])
                        for k in range(NCH):
                            S.op("pe", (lambda f, k, pu: lambda e: e.matmul(pu[:, 0:W], lhsT=wu[b][:, k, f * 128:(f + 1) * 128],
                                                                            rhs=xn[:, k, c0:c0 + W], start=(k == 0), stop=(k == NCH - 1)))(f, k, pu),
                                 reads=["wu%d" % b, "fnxn"], writes=[pnu])
                        sgi = (it + f) % 2
                        S.op("act", (lambda pg, sgi: lambda e: e.activation(out=sg[sgi][:, 0:W], in_=pg[:, 0:W], func=AF.Silu))(pg, sgi),
                             reads=[png], writes=["sg%d" % sgi])
                        S.op("dve", (lambda f, pu, sgi: lambda e: e.tensor_tensor(out=act[ab][:, f, 0:W], in0=pu[:, 0:W], in1=sg[sgi][:, 0:W],
                                                                                  op=ALU.mult))(f, pu, sgi),
                             reads=[pnu, "sg%d" % sgi], writes=["act%d" % ab])
                    for c in range(NCH):
                        po, pno = bank()
                        for f in range(fs):
                            S.op("pe", (lambda f, c, po: lambda e: e.matmul(po[:, 0:W], lhsT=wd[b][:, f, c * 128:(c + 1) * 128],
                                                                            rhs=act[ab][:, f, 0:W], start=(f == 0), stop=(f == fs - 1)))(f, c, po),
                                 reads=["wd%d" % b, "act%d" % ab], writes=[pno])
                        S.op("dve", (lambda c, po: lambda e: e.tensor_tensor(out=h[:, c, c0:c0 + W], in0=po[:, 0:W], in1=h[:, c, c0:c0 + W],
                                                                             op=ALU.add))(c, po),
                             reads=[pno] + hres(c0, W), writes=hres(c0, W))
                if s + 2 < len(SLABS):
                    load_slab(s + 2, f0s[s + 2])
            return hooked

        def aalloc_at(off_words, shape, dt, names, parts=128):
            save = ar["off"]
            ar["off"] = off_words
            v = aalloc(shape, dt, names, parts)
            ar["off"] = save
            return v

        PLE_W_OFF = arena_words - (NCH * D + 2 * D) // 2
        WIN_WORDS = NCH * GIN // 2
        WIN_OFF = 12352
        WOUT_OFF = WIN_OFF + WIN_WORDS
        GSM_OFF = WOUT_OFF + NCH * D // 2
        assert WOUT_OFF <= PLE_W_OFF

        def gla_pre():
            win = aalloc_at(WIN_OFF, [NCH, GIN], BF16, ["win"])
            S.op("pool", lambda e: e.dma_start(out=win, in_=w_gin.rearrange("(c p) n -> p c n", p=128)), writes=["win"], dma="win")
            return win

        def ple_pre(layer):
            wpg = aalloc_at(PLE_W_OFF, [NCH, D], BF16, ["wpg"])
            wpp = aalloc_at(PLE_W_OFF + NCH * D // 2, [2, D], BF16, ["wpp"])
            S.op("pool", lambda e: e.dma_start(out=wpg, in_=w_pg[layer].rearrange("(c p) n -> p c n", p=128)), writes=["wpg"], dma="wpg")
            S.op("pool", lambda e: e.dma_start(out=wpp, in_=w_pp[layer].rearrange("(c p) n -> p c n", p=128)), writes=["wpp"], dma="wpp")
            return wpg, wpp

        def zipper(*gens):
            alive = list(gens)
            while alive:
                for g in list(alive):
                    try:
                        next(g)
                    except StopIteration:
                        alive.remove(g)

        def drain(g):
            for _ in g:
                pass

        def ple(layer, p_tok, p_s, ntok, wts, hook=None):
            wpg, wpp = wts
            areset()
            xn2 = [aalloc([NCH, 512], BF16, ["pnxn%d" % i]) for i in range(2)]
            pT = aalloc([2, ntok], BF16, ["pT%d" % i for i in range(5)])
            ptb = [aalloc([2, 256], F32, ["ptb%d" % i]) for i in range(2)]
            sq = aalloc([NCH, 512], BF16, ["pnsq"]); lnv = aalloc([512], F32, ["pn_ln"]); rs = aalloc([512], F32, ["pnrs"])
            sgt = [aalloc([512], F32, ["sgt%d" % i]) for i in range(2)]
            tmp = [aalloc([512], F32, ["tmp%d" % i]) for i in range(2)]
            assert ar["off"] <= WIN_OFF, (ar["off"], WIN_OFF)
            tl = _tiles(ntok)
            hooked = hook() if hook is not None else None

            def gen_pT():
                for bb in range(SEG // 256):
                    sl = bb % 2
                    S.op("sp", lambda e: e.dma_start(out=ptb[sl], in_=p_tok[bb * 256:(bb + 1) * 256, :].rearrange("(j p) n -> p j n", p=128)),
                         writes=["ptb%d" % sl], dma="ptb%d" % sl)
                    pb, pn = bank()
                    for j in range(2):
                        for kc in range(2):
                            S.op("pe", lambda e: e.matmul(pb[:, (j * 2 + kc) * 128:(j * 2 + kc + 1) * 128], lhsT=ptb[sl][:, j, kc * 128:(kc + 1) * 128],
                                                          rhs=ident[:], start=True, stop=True), reads=["ptb%d" % sl, "ident"], writes=[pn])
                    for j in range(2):
                        col = bb * 256 + j * 128
                        S.op("act", lambda e: e.activation(out=pT[:, :, col:col + 128], in_=pb[:, j * 256:(j + 1) * 256].rearrange("p (a b) -> p a b", a=2),
                                                           func=AF.Copy), reads=[pn], writes=["pT%d" % (bb // 2)])
                    yield
                if p_s is not None:
                    pts = ptb[0][0:64, 0, :]
                    S.op("dve", lambda e: e.memset(pts, 0.0), writes=["ptb0"])
                    for j in range(2):
                        S.op("sp", lambda e: e.dma_start(out=ptb[0][32 * j:32 * j + 16, 0, :], in_=p_s[16 * j:16 * j + 16, :]),
                             writes=["ptb0"], dma="pts%d" % j)
                    pb, pn = bank()
                    for kc in range(2):
                        S.op("pe", lambda e: e.matmul(pb[:, kc * 64:(kc + 1) * 64], lhsT=pts[:, kc * 128:(kc + 1) * 128],
                                                      rhs=ident[0:64, 0:64], start=True, stop=True), reads=["ptb0", "ident"], writes=[pn])
                    S.op("act", lambda e: e.activation(out=pT[:, :, SEG:NT], in_=pb[:, 0:128].rearrange("p (a b) -> p a b", a=2), func=AF.Copy),
                         reads=[pn], writes=["pT4"])
                    yield

            def gen_norm(t):
                c0, W = tl[t]
                norm_fm(xn2[t % 2][:, :, 0:W], 2 + layer, c0, W, (sq, lnv, rs), "pn", xname="pnxn%d" % (t % 2))
                yield

            def gen_work(t):
                c0, W = tl[t]
                xs_ = xn2[t % 2]
                xr = "pnxn%d" % (t % 2)
                for c in range(NCH):
                    sl = c % 2
                    pg, png = bank()
                    pp, pnp = bank()
                    for k in range(NCH):
                        S.op("pe", lambda e: e.matmul(pg[:, 0:W], lhsT=wpg[:, k, c * 128:(c + 1) * 128], rhs=xs_[:, k, 0:W],
                                                      start=(k == 0), stop=(k == NCH - 1)), reads=["wpg", xr], writes=[png])
                    for k in range(2):
                        S.op("pe", lambda e: e.matmul(pp[:, 0:W], lhsT=wpp[:, k, c * 128:(c + 1) * 128], rhs=pT[:, k, c0:c0 + W],
                                                      start=(k == 0), stop=(k == 1)), reads=["wpp", "pT%d" % t], writes=[pnp])
                    S.op("act", lambda e: e.activation(out=sgt[sl][:, 0:W], in_=pg[:, 0:W], func=AF.Sigmoid), reads=[png], writes=["sgt%d" % sl])
                    S.op("dve", lambda e: e.tensor_tensor(out=tmp[sl][:, 0:W], in0=pp[:, 0:W], in1=sgt[sl][:, 0:W], op=ALU.mult),
                         reads=[pnp, "sgt%d" % sl], writes=["tmp%d" % sl])
                    S.op("dve", lambda e: e.tensor_tensor(out=h[:, c, c0:c0 + W], in0=h[:, c, c0:c0 + W], in1=tmp[sl][:, 0:W], op=ALU.add),
                         reads=["tmp%d" % sl] + hres(c0, W), writes=hres(c0, W))
                    yield

            gp = gen_pT()
            next(gp); next(gp)
            drain(gen_norm(0))

            def rest():
                for t in range(len(tl)):
                    if t + 1 < len(tl):
                        yield from gen_norm(t + 1)
                    yield from gen_work(t)

            zipper(gp, rest())
            return hooked

        def gla(ntok, full, win):
            areset()
            if full:
                wout = aalloc_at(WOUT_OFF, [NCH, D], BF16, ["wout"])
            xn = [aalloc([NCH, GT], BF16, ["gxn%d" % i]) for i in range(2)]
            lnv = aalloc([GT], F32, ["gn_ln"]); rs = aalloc([GT], F32, ["gnrs"])
            grT = aalloc([GT], BF16, ["grT"])
            mk = ar["off"]
            sq = aalloc([NCH, GT], BF16, ["gsq", "la", "e3"])
            la = arena[:, mk:mk + 512]
            e3 = arena[:, mk + 512:mk + 1024]
            kend = [aalloc([2, 512], BF16, ["kend%d_%d" % (i, j) for j in range(2)]) for i in range(2)]
            vtm = [aalloc([2, D], BF16, ["vtm%d_%d" % (i, j) for j in range(2)]) for i in range(2)]
            dec = [aalloc([8], F32, ["dec%d" % i]) for i in range(2)]
            if full:
                e1 = aalloc([GT], F32, ["e1_0", "e1_1"]); e2 = aalloc([GT], F32, ["e2_0", "e2_1"])
                qdec = [aalloc([4, GT], BF16, ["qdec%d_%d" % (i, j) for j in range(4)]) for i in range(2)]
                kinv = [aalloc([4, GT], BF16, ["kinv%d_%d" % (i, j) for j in range(4)]) for i in range(2)]
                sc = [aalloc([2, 128], BF16, ["sc%d_%d" % (i, j) for j in range(2)]) for i in range(2)]
                sgg = aalloc([NCH, GT], BF16, ["sgg%d" % i for i in range(8)])
                og = aalloc([NCH, GT], BF16, ["og%d" % i for i in range(8)])
                osq = aalloc([4, GT], BF16, ["osq0", "osq1"])
                save_ = ar["off"]
                ar["off"] = GSM_OFF
                rso = [aalloc([GT], F32, ["rso%d" % i]) for i in range(2)]
                lno = [aalloc([GT], F32, ["go_ln"])] * 2
                ogt = [aalloc([GT], F32, ["ogt"])] * 2
                ar["off"] = save_
            assert ar["off"] <= WIN_OFF, (ar["off"], WIN_OFF)
            if full:
                S.op("pool", lambda e: e.dma_start(out=wout, in_=w_gout.rearrange("(c p) n -> p c n", p=128)), writes=["wout"], dma="wout")
            QO, KO, VO, GO, RO = 0, 512, 1024, 2048, 3072
            tiles = [(c0, GT, "P") for c0 in range(0, SEG, GT)]
            if ntok > SEG:
                tiles.append((SEG, 64, "S"))
            rp = {"i": 0}
            rq = {"i": 0}

            def bankP():
                i = rp["i"] % 3
                rp["i"] += 1
                return pbank[i], "pb%d" % i

            def bankR():
                i = 6 + rq["i"] % 2
                rq["i"] += 1
                return pbank[i], "pb%d" % i

            def blocks_of(W, kind):
                if kind == "P":
                    return [(bi * 128, 128, [(0, 128, 0)], tri, um, cm) for bi in range(W // 128)]
                return [(0, 64, [(0, 16, 1), (32, 16, 2)], tris, ums, cms)]

            def Pn(t):
                c0, W, kind = tiles[t]
                s = t % 2
                xs_ = xn[s % len(xn)]
                hr = hres(c0, W)
                S.op("act", lambda e: e.activation(out=sq[:, :, 0:W], in_=h[:, :, c0:c0 + W], func=AF.Square), reads=hr, writes=["gsq", "la", "e3"])
                pb, pn = bankP()
                for c in range(NCH):
                    S.op("pe", lambda e: e.matmul(pb[:, 0:W], lhsT=ones_b[:], rhs=sq[:, c, 0:W], start=(c == 0), stop=(c == NCH - 1)),
                         reads=["gsq", "la", "e3"], writes=[pn])
                rstd_from_ss(pb[:, 0:W], rs[:, 0:W], float(D), "gn", [pn], ["gnrs"], lnv[:, 0:W])
                for c in range(NCH):
                    S.op("pool", lambda e: e.tensor_scalar_mul(out=lnv[:, 0:W], in0=h[:, c, c0:c0 + W], scalar1=cols[:, 4, c:c + 1]),
                         reads=hr + ["gnrs"], writes=["gn_ln"])
                    S.op("pool", lambda e: e.tensor_tensor(out=xs_[:, c, 0:W], in0=lnv[:, 0:W], in1=rs[:, 0:W], op=ALU.mult),
                         reads=["gn_ln", "gnrs"], writes=["gxn%d" % (s % len(xn))])
                yield

            def P(t, with_norm=True):
                c0, W, kind = tiles[t]
                s = t % 2
                xs_ = xn[s % len(xn)]
                hr = hres(c0, W)
                blocks = blocks_of(W, kind)
                if with_norm:
                    yield from Pn(t)
                xr = "gxn%d" % (s % len(xn))
                yield
                pb, pn = bankP()
                for k in range(NCH):
                    S.op("pe", lambda e: e.matmul(pb[0:16, 0:W], lhsT=win[:, k, RO:RO + 16], rhs=xs_[:, k, 0:W], start=(k == 0), stop=(k == NCH - 1)),
                         reads=["win", xr], writes=[pn])
                S.op("act", lambda e: e.activation(out=grT[0:16, 0:W], in_=pb[0:16, 0:W], func=AF.Copy), reads=[pn], writes=["grT"])
                pdec, pndec = pbank[3], "pb3"
                yield
                ndec = 0
                for bi, (b0, BW, subs, trim, umm, cmm) in enumerate(blocks):
                    R = BW
                    for vh in range(2):
                        pv, pnv = bankP()
                        for k in range(NCH):
                            S.op("pe", lambda e: e.matmul(pv[0:R, :], lhsT=xs_[:, k, b0:b0 + BW], rhs=win[:, k, VO + vh * 512:VO + (vh + 1) * 512],
                                                          start=(k == 0), stop=(k == NCH - 1)), reads=["win", xr], writes=[pnv])
                        S.op("act", lambda e: e.activation(out=vtm[s][0:R, bi, vh * 512:(vh + 1) * 512], in_=pv[0:R, :], func=AF.Copy),
                             reads=[pnv], writes=["vtm%d_%d" % (s, bi)])
                        yield
                    pb, pn = bankP()
                    S.op("pe", lambda e: e.matmul(pb[0:R, :], lhsT=grT[0:16, b0:b0 + BW], rhs=wgu[:], start=True, stop=False), reads=["grT", "wgu"], writes=[pn])
                    S.op("pe", lambda e: e.matmul(pb[0:R, :], lhsT=ones_row[0:1, 0:BW], rhs=bgate[:], start=False, stop=True), reads=["bgate", "ones_row"], writes=[pn])
                    S.op("act", lambda e: e.activation(out=la[0:R, :], in_=pb[0:R, :], func=AF.Exp, scale=-1.0), reads=[pn], writes=["la"])
                    S.op("act", lambda e: e.activation(out=la[0:R, :], in_=la[0:R, :], func=AF.Ln, bias=1.0), reads=["la"], writes=["la"])
                    yield
                    pk, pnk = bankP()
                    for k in range(NCH):
                        S.op("pe", lambda e: e.matmul(pk[0:R, :], lhsT=xs_[:, k, b0:b0 + BW], rhs=win[:, k, KO:KO + 512], start=(k == 0), stop=(k == NCH - 1)),
                             reads=["win", xr], writes=[pnk])
                    pb, pn = bankP()
                    S.op("pe", lambda e: e.matmul(pb[0:R, :], lhsT=umm[0:R, 0:R], rhs=la[0:R, :], start=True, stop=True), reads=["la", "um"], writes=[pn])
                    S.op("act", lambda e: e.activation(out=e3[0:R, :], in_=pb[0:R, :], func=AF.Exp), reads=[pn], writes=["e3"])
                    S.op("dve", lambda e: e.tensor_tensor(out=kend[s][0:R, bi, :], in0=pk[0:R, :], in1=e3[0:R, :], op=ALU.mult),
                         reads=[pnk, "e3"], writes=["kend%d_%d" % (s, bi)])
                    yield
                    for (pbase, ln, sid) in subs:
                        for hd in range(4):
                            col = ndec * 4 + hd
                            S.op("pe", lambda e: e.matmul(pdec[:, col:col + 1], lhsT=la[pbase:pbase + ln, hd * 128:(hd + 1) * 128],
                                                          rhs=m16[pbase:pbase + ln, 0:1], start=True, stop=True), reads=["la", "m16"], writes=[pndec])
                        ndec += 1
                    if full:
                        for hd in range(4):
                            pbT, pnT = bankP()
                            S.op("pe", lambda e: e.matmul(pbT[:, 0:BW], lhsT=la[0:BW, hd * 128:(hd + 1) * 128], rhs=trim[0:BW, 0:BW], start=True, stop=True),
                                 reads=["la", "tri"], writes=[pnT])
                            S.op("act", lambda e: e.activation(out=e1[:, hd % 2 * 128:hd % 2 * 128 + BW], in_=pbT[:, 0:BW], func=AF.Exp), reads=[pnT], writes=["e1_%d" % (hd % 2)])
                            S.op("act", lambda e: e.activation(out=e2[:, hd % 2 * 128:hd % 2 * 128 + BW], in_=pbT[:, 0:BW], func=AF.Exp, scale=-1.0), reads=[pnT], writes=["e2_%d" % (hd % 2)])
                            pq, pnq = bankP()
                            for k in range(NCH):
                                S.op("pe", lambda e: e.matmul(pq[:, 0:BW], lhsT=win[:, k, QO + hd * 128:QO + (hd + 1) * 128], rhs=xs_[:, k, b0:b0 + BW],
                                                              start=(k == 0), stop=(k == NCH - 1)), reads=["win", xr], writes=[pnq])
                            for k in range(NCH):
                                S.op("pe", lambda e: e.matmul(pq[:, 128:128 + BW], lhsT=win[:, k, KO + hd * 128:KO + (hd + 1) * 128], rhs=xs_[:, k, b0:b0 + BW],
                                                              start=(k == 0), stop=(k == NCH - 1)), reads=["win", xr], writes=[pnq])
                            S.op("dve", lambda e: e.scalar_tensor_tensor(out=qdec[s][:, hd, b0:b0 + BW], in0=pq[:, 0:BW], scalar=128.0 ** -0.5,
                                                                         in1=e1[:, hd % 2 * 128:hd % 2 * 128 + BW], op0=ALU.mult, op1=ALU.mult),
                                 reads=[pnq, "e1_%d" % (hd % 2)], writes=["qdec%d_%d" % (s, hd)])
                            S.op("dve", lambda e: e.tensor_tensor(out=kinv[s][:, hd, b0:b0 + BW], in0=pq[:, 128:128 + BW], in1=e2[:, hd % 2 * 128:hd % 2 * 128 + BW], op=ALU.mult),
                                 reads=[pnq, "e2_%d" % (hd % 2)], writes=["kinv%d_%d" % (s, hd)])
                            yield
                S.op("act", lambda e: e.activation(out=dec[s][:, 0:ndec * 4], in_=pdec[:, 0:ndec * 4], func=AF.Exp), reads=[pndec], writes=["dec%d" % s])

            def Rr(t):
                c0, W, kind = tiles[t]
                s = t % 2
                xs_ = xn[s % len(xn)]
                xr = "gxn%d" % (s % len(xn))
                hr = hres(c0, W)
                blocks = blocks_of(W, kind)
                if full:
                    for c in range(NCH):
                        pg, png = bankR()
                        for k in range(NCH):
                            S.op("pe", lambda e: e.matmul(pg[:, 0:W], lhsT=win[:, k, GO + c * 128:GO + (c + 1) * 128], rhs=xs_[:, k, 0:W],
                                                          start=(k == 0), stop=(k == NCH - 1)), reads=["win", xr], writes=[png])
                        S.op("act", lambda e: e.activation(out=sgg[:, c, 0:W], in_=pg[:, 0:W], func=AF.Silu), reads=[png], writes=["sgg%d" % c])
                    yield
                for pair in range(2):
                    heads = (2 * pair, 2 * pair + 1)
                    po = {}
                    if full:
                        for hh_, hd in enumerate(heads):
                            po[hd] = (pbank[4 + hh_], "pb%d" % (4 + hh_))
                    di = 0
                    for bi, (b0, BW, subs, trim, umm, cmm) in enumerate(blocks):
                        if full:
                            psc, pnsc = bankR()
                            si = (pair * 2 + bi) % 2
                            for hh, hd in enumerate(heads):
                                S.op("pe", lambda e: e.matmul(psc[0:BW, hh * 128:hh * 128 + BW], lhsT=kinv[s][:, hd, b0:b0 + BW], rhs=qdec[s][:, hd, b0:b0 + BW],
                                                              start=True, stop=True), reads=["kinv%d_%d" % (s, hd), "qdec%d_%d" % (s, hd)], writes=[pnsc])
                            for hh, hd in enumerate(heads):
                                S.op("dve", lambda e: e.tensor_tensor(out=sc[si][0:BW, hh, 0:BW], in0=psc[0:BW, hh * 128:hh * 128 + BW], in1=cmm[0:BW, 0:BW], op=ALU.mult),
                                     reads=[pnsc, "cm"], writes=["sc%d_%d" % (si, hh)])
                            yield
                            for hh, hd in enumerate(heads):
                                pob, pno = po[hd]
                                for dvc in range(2):
                                    oc = dvc * GT + b0
                                    S.op("pe", lambda e: e.matmul(pob[:, oc:oc + BW], lhsT=vtm[s][0:BW, bi, hd * 256 + dvc * 128:hd * 256 + (dvc + 1) * 128],
                                                                  rhs=sc[si][0:BW, hh, 0:BW], start=True, stop=False, skip_group_check=True),
                                         reads=["vtm%d_%d" % (s, bi), "sc%d_%d" % (si, hh)], writes=[pno])
                                    for sj, (pbase, ln, sid) in enumerate(subs):
                                        slot = sid
                                        lastsub = sj == len(subs) - 1
                                        S.op("pe", lambda e: e.matmul(pob[:, oc + pbase:oc + pbase + ln], lhsT=Sbf[:, slot, hd, dvc * 128:(dvc + 1) * 128],
                                                                      rhs=qdec[s][:, hd, b0 + pbase:b0 + pbase + ln], start=False, stop=lastsub, skip_group_check=True),
                                             reads=["Sbf%d_%d" % (slot, hd), "qdec%d_%d" % (s, hd)], writes=[pno])
                        yield
                        for sj, (pbase, ln, sid) in enumerate(subs):
                            pu, pnu = bankR()
                            for hh, hd in enumerate(heads):
                                S.op("pe", lambda e: e.matmul(pu[:, hh * 256:(hh + 1) * 256], lhsT=kend[s][pbase:pbase + ln, bi, hd * 128:(hd + 1) * 128],
                                                              rhs=vtm[s][pbase:pbase + ln, bi, hd * 256:(hd + 1) * 256], start=True, stop=True),
                                     reads=["kend%d_%d" % (s, bi), "vtm%d_%d" % (s, bi)], writes=[pnu])
                            for hh, hd in enumerate(heads):
                                dcol = (di + sj) * 4 + hd
                                S.op("dve", lambda e: e.scalar_tensor_tensor(out=Sst[:, sid, hd, :], in0=Sst[:, sid, hd, :], scalar=dec[s][:, dcol:dcol + 1],
                                                                             in1=pu[:, hh * 256:(hh + 1) * 256], op0=ALU.mult, op1=ALU.add),
                                     reads=[pnu, "dec%d" % s, "Sst%d_%d" % (sid, hd)], writes=["Sst%d_%d" % (sid, hd)])
                                if full and sid == 0:
                                    S.op("act", lambda e: e.activation(out=Sbf[:, 0, hd, :], in_=Sst[:, 0, hd, :], func=AF.Copy),
                                         reads=["Sst0_%d" % hd], writes=["Sbf0_%d" % hd])
                        di += len(subs)
                        yield
                    if full:
                        for hh, hd in enumerate(heads):
                            pob, pno = po[hd]
                            S.op("act", lambda e: e.activation(out=osq[:, hh * 2:hh * 2 + 2, 0:W], in_=pob[:].rearrange("p (a b) -> p a b", a=2)[:, :, 0:W], func=AF.Square),
                                 reads=[pno], writes=["osq%d" % hh])
                            pb, pn = bankR()
                            for dvc in range(2):
                                S.op("pe", lambda e: e.matmul(pb[:, 0:W], lhsT=ones_b[:], rhs=osq[:, hh * 2 + dvc, 0:W], start=(dvc == 0), stop=(dvc == 1)),
                                     reads=["osq%d" % hh, "ones_b"], writes=[pn])
                            rstd_from_ss(pb[:, 0:W], rso[hh][:, 0:W], 256.0, "go", [pn], ["rso%d" % hh], lno[hh][:, 0:W])
                            yield
                        for hh, hd in enumerate(heads):
                            pob, pno = po[hd]
                            for dvc in range(2):
                                c = hd * 2 + dvc
                                ti = dvc
                                S.op("dve", lambda e: e.scalar_tensor_tensor(out=ogt[ti][:, 0:W], in0=pob[:, dvc * GT:dvc * GT + W], scalar=gn_col[:, dvc:dvc + 1],
                                                                             in1=rso[hh][:, 0:W], op0=ALU.mult, op1=ALU.mult),
                                     reads=[pno, "rso%d" % hh], writes=["ogt"])
                                S.op("pool", lambda e: e.tensor_tensor(out=og[:, c, 0:W], in0=ogt[ti][:, 0:W], in1=sgg[:, c, 0:W], op=ALU.mult),
                                     reads=["ogt", "sgg%d" % c], writes=["og%d" % c])
                        yield
                if full:
                    for c in range(NCH):
                        pb, pn = bankR()
                        for k in range(NCH):
                            S.op("pe", lambda e: e.matmul(pb[:, 0:W], lhsT=wout[:, k, c * 128:(c + 1) * 128], rhs=og[:, k, 0:W], start=(k == 0), stop=(k == NCH - 1)),
                                 reads=["wout", "og%d" % k], writes=[pn])
                        S.op("dve", lambda e: e.tensor_tensor(out=h[:, c, c0:c0 + W], in0=pb[:, 0:W], in1=h[:, c, c0:c0 + W], op=ALU.add),
                             reads=[pn] + hr, writes=hr)
                        if c % 2 == 1:
                            yield

            drain(Pn(0))
            drain(P(0, False))
            if len(tiles) > 1:
                drain(Pn(1))
            for t in range(len(tiles)):
                gens = [Rr(t)]
                if t + 1 < len(tiles):
                    gens.append(P(t + 1, False))
                if t + 2 < len(tiles):
                    gens.append(Pn(t + 2))
                zipper(*gens)

        def final(ntok):
            areset()
            gbc = aalloc([D], F32, ["gfin"])
            yt = [aalloc([D], F32, ["yt%d" % i]) for i in range(2)]
            junk = aalloc([D], BF16, ["junkf"])
            ld(gbc, norm_final.partition_broadcast(128), "gfin")
            tl = [(i * 128, 128) for i in range(SEG // 128)] + ([(SEG, 64)] if ntok > SEG else [])
            for i, (c0, W) in enumerate(tl):
                tt = c0 // 512
                sl = i % 2
                pbs = []
                for half in range(2):
                    pb, pn = bank()
                    pbs.append((pb, pn))
                    for q in range(4):
                        c = half * 4 + q
                        S.op("pe", (lambda c, q, pb: lambda e: e.matmul(pb[0:W, q * 128:(q + 1) * 128], lhsT=h[:, c, c0:c0 + W], rhs=ident[:],
                                                                        start=True, stop=True))(c, q, pb),
                             reads=hres(c0, W) + ["ident"], writes=[pn])
                S.op("pool", lambda e: e.memset(ssq_t[:, 0:2], 0.0), writes=["ssq0", "ssq1"])
                for half in range(2):
                    pb, pn = pbs[half]
                    S.op("act", (lambda half, pb: lambda e: e.activation(out=junk[0:W, half * 512:(half + 1) * 512], in_=pb[0:W, :], func=AF.Square,
                                                                         accum_out=ssq_t[0:W, half:half + 1]))(half, pb),
                         reads=[pn, "ssq%d" % half], writes=["ssq%d" % half, "junkf"])
                S.op("dve", lambda e: e.tensor_tensor(out=ssq2_t[0:W, 0:1], in0=ssq_t[0:W, 0:1], in1=ssq_t[0:W, 1:2], op=ALU.add),
                     reads=["ssq0", "ssq1"], writes=["ssq2"])
                rstd_from_ss(ssq2_t[0:W, 0:1], rs_t[0:W, 0:1], float(D), "fin", ["ssq2"], ["finrs"], ln_t[0:W, 0:1])
                for half in range(2):
                    pb, pn = pbs[half]
                    S.op("dve", (lambda half, pb, sl: lambda e: e.scalar_tensor_tensor(out=yt[sl][0:W, half * 512:(half + 1) * 512], in0=pb[0:W, :],
                                                                                       scalar=rs_t[0:W, 0:1], in1=gbc[0:W, half * 512:(half + 1) * 512],
                                                                                       op0=ALU.mult, op1=ALU.mult))(half, pb, sl),
                         reads=[pn, "finrs", "gfin"], writes=["yt%d" % sl])
                if W == 128:
                    S.op("sp", (lambda c0, sl: lambda e: e.dma_start(out=y_main[c0:c0 + 128, :], in_=yt[sl][:, :]))(c0, sl), reads=["yt%d" % sl], dma="o_y%d" % sl)
                else:
                    for j in range(2):
                        S.op("sp", (lambda j, sl: lambda e: e.dma_start(out=y_smp[16 * j:16 * j + 16, :], in_=yt[sl][32 * j:32 * j + 16, :]))(j, sl),
                             reads=["yt%d" % sl], dma="o_ys%d" % j)

        stages = []
        S.op("dve", lambda e: e.memset(Sst[:, 0, :, :], 0.0), writes=["Sst0_%d" % i for i in range(4)])
        l0_mixer(x_warm, x_wprev, c_band0w, SEG, False, False)
        wts = ffn(0, SEG, lambda: ple_pre(0))
        win_ = ple(0, p_warm, None, SEG, wts, gla_pre)
        gla(SEG, False, win_)
        for hd in range(4):
            S.op("dve", (lambda hd: lambda e: e.tensor_scalar_mul(out=Sst[:, 0, hd, :], in0=Sst[:, 0, hd, :], scalar1=useprev[:, 0:1]))(hd),
                 reads=["Sst0_%d" % hd, "useprev"], writes=["Sst0_%d" % hd])
            S.op("act", (lambda hd: lambda e: e.activation(out=Sbf[:, 0, hd, :], in_=Sst[:, 0, hd, :], func=AF.Copy))(hd),
                 reads=["Sst0_%d" % hd], writes=["Sbf0_%d" % hd])
        for j in range(2):
            S.op("sp", (lambda j: lambda e: e.dma_start(out=Sst[:, 1 + j, :, :], in_=sgla_in[j].rearrange("h k v -> k h v")))(j),
                 writes=["Sst%d_%d" % (1 + j, i) for i in range(4)], dma="sgla%d" % j)
            for hd in range(4):
                S.op("act", (lambda j, hd: lambda e: e.activation(out=Sbf[:, 1 + j, hd, :], in_=Sst[:, 1 + j, hd, :], func=AF.Copy))(j, hd),
                     reads=["Sst%d_%d" % (1 + j, hd)], writes=["Sbf%d_%d" % (1 + j, hd)])
        S.fence()
        l0_mixer(x_main, x_mprev, band0m_in, NT, True, True)
        wts = ffn(0, NT, lambda: ple_pre(0))
        win_ = ple(0, p_main[0], p_smp[0], NT, wts, gla_pre)
        gla(NT, True, win_)
        S.op("sp", lambda e: e.dma_start(out=gs_main.rearrange("h k v -> k h v"), in_=Sst[:, 0, :, :]),
             reads=["Sst0_%d" % i for i in range(4)], dma="o_gsm")
        for j in range(2):
            S.op("sp", (lambda j: lambda e: e.dma_start(out=gs_smp[j].rearrange("h k v -> k h v"), in_=Sst[:, 1 + j, :, :]))(j),
                 reads=["Sst%d_%d" % (1 + j, i) for i in range(4)], dma="o_gss%d" % j)
        wts = ffn(1, NT, lambda: ple_pre(1))
        ple(1, p_main[1], p_smp[1], NT, wts)
        final(NT)
        counts = S.emit()
    return nc, counts


def _consts():
    c = {}
    c["c_ident"] = np.eye(128, dtype=np.float32)
    sel = np.zeros((64, 64), np.float32)
    for j in range(2):
        for t in range(16):
            sel[32 * j + 15 + t, 32 * j + t] = 1.0
    c["c_sel"] = sel

    def band_mats(start):
        band = np.zeros((4, 128, 128), np.float32)
        halo = np.zeros((4, 128, 128), np.float32)
        for wi, w in enumerate(WINS):
            for t in range(128):
                cnt = w if start is None else min(w, start + t + 1)
                for i in range(w):
                    s = t - i
                    if s >= 0:
                        band[wi, s, t] += 1.0 / cnt
                    else:
                        halo[wi, 128 + s, t] += 1.0 / cnt
                band[wi, t, t] -= 1.0
        return band, halo

    band, halo = band_mats(None)
    c["c_band"] = band
    c["c_halo"] = halo
    c["c_band0w"] = band_mats(0)[0]
    bs = np.zeros((4, 64, 64), np.float32)
    for wi, w in enumerate(WINS):
        for j in range(2):
            for t in range(16):
                for i in range(w):
                    bs[wi, 32 * j + 15 + t - i, 32 * j + t] += 1.0 / w
                bs[wi, 32 * j + 15 + t, 32 * j + t] -= 1.0
    c["c_bands"] = bs
    m = np.arange(128)[:, None]
    l = np.arange(128)[None, :]
    c["c_tri"] = ((m <= l) * (-1.0 / 16.0)).astype(np.float32)
    c["c_u"] = ((m > l) * (-1.0 / 16.0)).astype(np.float32)
    c["c_cm"] = (m <= l).astype(np.float32)
    m = np.arange(64)[:, None]
    l = np.arange(64)[None, :]
    valid = ((m % 32) < 16) & ((l % 32) < 16) & ((m // 32) == (l // 32))
    c["c_tris"] = ((valid & (m <= l)) * (-1.0 / 16.0)).astype(np.float32)
    c["c_us"] = ((valid & (m > l)) * (-1.0 / 16.0)).astype(np.float32)
    c["c_cms"] = (valid & (m <= l)).astype(np.float32)
    c["band0_first"] = band_mats(0)[0]
    return c


_CACHE = {}


def kernel(**inputs):
    f = lambda a: np.ascontiguousarray(np.asarray(a, dtype=np.float32))
    inp = {k: f(v) for k, v in inputs.items()}
    if "nc" not in _CACHE:
        _CACHE["nc"] = build_program()[0]
    nc = _CACHE["nc"]
    cst = _consts()
    band0_first = cst.pop("band0_first")
    xp, xsm = inp["x_prompt"], inp["x_sample"]
    pp, psm = inp["p_prompt"], inp["p_sample"]
    zeros_tile = np.zeros((128, D), np.float32)
    in_maps = []
    for c in range(8):
        b, half = c // 2, c % 2
        m = dict(cst)
        m["x_main"] = f(xp[b, half * SEG:(half + 1) * SEG])
        m["x_mprev"] = f(xp[b, SEG - 128:SEG]) if half == 1 else zeros_tile
        m["x_warm"] = f(xp[b, 0:SEG])
        m["x_wprev"] = zeros_tile
        m["xs"] = f(xsm[2 * c:2 * c + 2].reshape(32, D))
        m["hist"] = f(inp["state_pool"][0, 2 * c:2 * c + 2].reshape(30, D))
        m["sgla"] = f(inp["state_gla"][0, 2 * c:2 * c + 2])
        m["p_main"] = f(pp[:, b, half * SEG:(half + 1) * SEG])
        m["p_warm"] = f(pp[0, b, 0:SEG])
        m["p_smp"] = f(psm[:, 2 * c:2 * c + 2].reshape(2, 32, 256))
        m["useprev"] = np.full((128, 1), float(half), np.float32)
        m["band0m"] = band0_first if half == 0 else cst["c_band"]
        m["norm_mix"] = inp["norm_mix"]; m["norm_ffn"] = inp["norm_ffn"]; m["norm_ple"] = inp["norm_ple"]
        m["norm_final"] = inp["norm_final"]
        m["w_pool"] = f(inp["w_pool"][0]); m["b_pool"] = f(inp["b_pool"][0]); m["pool_scale"] = f(inp["pool_scale"][0])
        m["w_gla_in"] = f(inp["w_gla_in"][0]); m["w_gla_gate_up"] = f(inp["w_gla_gate_up"][0]); m["b_gla_gate"] = f(inp["b_gla_gate"][0])
        m["gla_norm"] = f(inp["gla_norm"][0]); m["w_gla_out"] = f(inp["w_gla_out"][0])
        m["w_ffn_gate"] = inp["w_ffn_gate"]; m["w_ffn_up"] = inp["w_ffn_up"]; m["w_ffn_down"] = inp["w_ffn_down"]
        m["w_ple_proj"] = inp["w_ple_proj"]; m["w_ple_gate"] = inp["w_ple_gate"]
        in_maps.append(m)
    res = run_bass_kernel_spmd(nc, in_maps, core_ids=list(range(8)))
    R = res.results
    y_prompt = np.stack([np.concatenate([R[2 * b]["y_main"], R[2 * b + 1]["y_main"]], 0) for b in range(4)], 0)
    y_sample = np.concatenate([R[c]["y_smp"].reshape(2, 16, D) for c in range(8)], 0)
    pool_p = np.stack([R[2 * b + 1]["ps_main"] for b in range(4)], 0)[None]
    pool_s = np.concatenate([R[c]["ps_smp"] for c in range(8)], 0)[None]
    gla_p = np.stack([R[2 * b + 1]["gs_main"] for b in range(4)], 0)[None]
    gla_s = np.concatenate([R[c]["gs_smp"] for c in range(8)], 0)[None]
    return tuple(np.ascontiguousarray(a.astype(np.float32)) for a in (y_prompt, y_sample, pool_p, pool_s, gla_p, gla_s))
```

```python
import contextlib
import types
import numpy as np
import concourse.bass as bass
import concourse.mybir as mybir
from concourse.bass_utils import run_bass_kernel_spmd

F32 = mybir.dt.float32
BF16 = mybir.dt.bfloat16
AF = mybir.ActivationFunctionType
ALU = mybir.AluOpType

D = 1024
NCH = 8
SEG = 2048
NT = 2112
DFF = 2816
NF = 22
GIN = 3088
EPS = 1e-6
WINS = (2, 4, 8, 16)
SLABS = (4, 4, 4, 4, 3, 3)
GT = 256
ATTACH_WAIT = True
PERSIST = ("h:", "pb", "cols", "ident", "tri", "um", "cm", "wgu", "bgate", "ones", "m16", "kc", "useprev", "gn_col", "Sst", "Sbf", "ssq", "l0n_ln", "l0rs", "fin_ln", "finrs")
SAME_ENGINE_SYNC = True


class Sched:
    ENGS = ("pe", "act", "dve", "pool", "sp")

    def __init__(self, nc):
        self.nc = nc
        self.ops = {e: [] for e in self.ENGS}
        self.res = {}
        self.last_dma = {}
        self.seq = 0
        self.known = set()
        self.check = False

    @staticmethod
    def _freeze(fn):
        if fn is None or fn.__closure__ is None:
            return fn
        cells = []
        for c in fn.__closure__:
            try:
                v = c.cell_contents
            except ValueError:
                cells.append(c)
                continue
            if isinstance(v, types.FunctionType):
                v = Sched._freeze(v)
            cells.append(types.CellType(v))
        return types.FunctionType(fn.__code__, fn.__globals__, fn.__name__, fn.__defaults__, tuple(cells))

    def op(self, eng, fn, reads=(), writes=(), dma=None, ndma=1):
        fn = self._freeze(fn)
        if self.check:
            for nm in list(reads) + list(writes):
                assert nm in self.known or nm.startswith(PERSIST), ("unregistered resource", nm)
        rec = {"eng": eng, "fn": fn, "deps": [], "inc": False, "dma": dma, "ndma": ndma, "tok": None, "seq": self.seq}
        self.seq += 1
        deps = []
        for r in reads:
            st = self.res.setdefault(r, {"w": None, "r": []})
            if st["w"] is not None:
                deps.append(st["w"])
        for w in writes:
            st = self.res.setdefault(w, {"w": None, "r": []})
            if st["w"] is not None:
                deps.append(st["w"])
            deps.extend(st["r"])
        for r in reads:
            self.res[r]["r"].append(rec)
        for w in writes:
            st = self.res[w]
            st["w"] = rec
            st["r"] = []
        latest = {}
        keep = []
        for d in deps:
            if d is rec:
                continue
            if d["dma"] is not None:
                keep.append(d)
                continue
            if d["eng"] == eng and (not SAME_ENGINE_SYNC or eng in ("pe", "sp")):
                continue
            cur = latest.get(d["eng"])
            if cur is None or d["seq"] > cur["seq"]:
                latest[d["eng"]] = d
        seen = set()
        for d in keep + list(latest.values()):
            if id(d) in seen:
                continue
            seen.add(id(d))
            rec["deps"].append(d)
            d["inc"] = True
        if dma is not None:
            rec["inc"] = True
            self.last_dma[dma] = rec
        self.ops[eng].append(rec)
        return rec

    def fence(self):
        lasts = []
        for e in self.ENGS:
            for rec in reversed(self.ops[e]):
                if rec["dma"] is None and rec["fn"] is not None:
                    lasts.append(rec)
                    break
        lasts.extend(self.last_dma.values())
        for e in self.ENGS:
            rec = {"eng": e, "fn": None, "deps": [], "inc": False, "dma": None, "ndma": 0, "tok": None, "seq": self.seq}
            self.seq += 1
            for d in lasts:
                if d["eng"] == e and d["dma"] is None and (not SAME_ENGINE_SYNC or e in ("pe", "sp")):
                    continue
                rec["deps"].append(d)
                d["inc"] = True
            self.ops[e].append(rec)
        self.res = {}

    def emit(self):
        nc = self.nc
        counts = {}
        for e in self.ENGS:
            for rec in self.ops[e]:
                if not rec["inc"]:
                    continue
                key = ("dma", rec["dma"]) if rec["dma"] is not None else ("eng", e)
                step = 16 * rec["ndma"] if rec["dma"] is not None else 1
                counts[key] = counts.get(key, 0) + step
                rec["tok"] = (key, counts[key])
        with contextlib.ExitStack() as st:
            sems = {}
            for i, k in enumerate(counts.keys()):
                sems[k] = st.enter_context(nc.semaphore("s%d" % i))
            block = st.enter_context(nc.Block())
            engmap = {"pe": block.tensor, "act": block.scalar, "dve": block.vector,
                      "pool": block.gpsimd, "sp": block.sync}

            def make(e):
                def body(eng):
                    waited = {}
                    for rec in self.ops[e]:
                        need = []
                        for d in rec["deps"]:
                            k, v = d["tok"]
                            if waited.get(k, 0) >= v:
                                continue
                            waited[k] = v
                            need.append((k, v))
                        attach = None
                        if ATTACH_WAIT and need and rec["fn"] is not None and rec["dma"] is None:
                            attach = need.pop()
                        for k, v in need:
                            eng.wait_ge(sems[k], v)
                        if rec["fn"] is None:
                            continue
                        ins = rec["fn"](eng)
                        if attach is not None:
                            ins._wait_ge(sems[attach[0]], attach[1])
                        if rec["inc"]:
                            if rec["dma"] is not None:
                                lst = ins if isinstance(ins, (list, tuple)) else [ins]
                                assert len(lst) == rec["ndma"], (len(lst), rec["ndma"])
                                for x in lst:
                                    x.then_inc(sems[rec["tok"][0]], 16)
                            else:
                                ins.then_inc(sems[rec["tok"][0]], 1)
                    if e == "sp":
                        for k, v in counts.items():
                            if k[0] == "dma" and waited.get(k, 0) < v:
                                eng.wait_ge(sems[k], v)
                return body

            for e in self.ENGS:
                engmap[e](make(e))
        return {e: len(self.ops[e]) for e in self.ENGS}


def _tiles(n):
    out = []
    c = 0
    while c < min(n, SEG):
        out.append((c, 512))
        c += 512
    if n > SEG:
        out.append((SEG, n - SEG))
    return out


def build_program(stop_stage=None):
    nc = bass.Bass("TRN2", target_bir_lowering=False)

    def din(name, shape):
        return nc.dram_tensor(name, list(shape), F32, kind="ExternalInput").ap()

    def dout(name, shape):
        return nc.dram_tensor(name, list(shape), F32, kind="ExternalOutput").ap()

    x_main = din("x_main", [SEG, D]); x_mprev = din("x_mprev", [128, D])
    x_warm = din("x_warm", [SEG, D]); x_wprev = din("x_wprev", [128, D])
    xs_in = din("xs", [32, D]); hist_in = din("hist", [30, D]); sgla_in = din("sgla", [2, 4, 128, 256])
    p_main = din("p_main", [2, SEG, 256]); p_warm = din("p_warm", [SEG, 256]); p_smp = din("p_smp", [2, 32, 256])
    useprev_in = din("useprev", [128, 1])
    band0m_in = din("band0m", [4, 128, 128])
    c_ident = din("c_ident", [128, 128]); c_sel = din("c_sel", [64, 64])
    c_band = din("c_band", [4, 128, 128]); c_halo = din("c_halo", [4, 128, 128]); c_band0w = din("c_band0w", [4, 128, 128])
    c_bands = din("c_bands", [4, 64, 64])
    c_tri = din("c_tri", [128, 128]); c_u = din("c_u", [128, 128]); c_cm = din("c_cm", [128, 128])
    c_tris = din("c_tris", [64, 64]); c_us = din("c_us", [64, 64]); c_cms = din("c_cms", [64, 64])
    norm_mix = din("norm_mix", [2, D]); norm_ffn = din("norm_ffn", [2, D]); norm_ple = din("norm_ple", [2, D])
    norm_final = din("norm_final", [D])
    w_pool = din("w_pool", [4, 256, 256]); b_pool = din("b_pool", [D]); pool_scale = din("pool_scale", [D])
    w_gin = din("w_gla_in", [D, GIN]); w_ggu = din("w_gla_gate_up", [16, 512]); b_gg = din("b_gla_gate", [512])
    gla_norm = din("gla_norm", [256]); w_gout = din("w_gla_out", [D, D])
    w_fg = din("w_ffn_gate", [2, D, DFF]); w_fu = din("w_ffn_up", [2, D, DFF]); w_fd = din("w_ffn_down", [2, DFF, D])
    w_pp = din("w_ple_proj", [2, 256, D]); w_pg = din("w_ple_gate", [2, D, D])
    y_main = dout("y_main", [SEG, D]); y_smp = dout("y_smp", [32, D])
    ps_main = dout("ps_main", [15, D]); ps_smp = dout("ps_smp", [2, 15, D])
    gs_main = dout("gs_main", [4, 128, 256]); gs_smp = dout("gs_smp", [2, 4, 128, 256])

    S = Sched(nc)
    S.check = True
    st = contextlib.ExitStack()
    with st:
        def sb(name, shape, dt):
            return st.enter_context(nc.sbuf_tensor(name, list(shape), dt))

        pbank = [st.enter_context(nc.psum_tensor("pb%d" % i, [128, 512], F32)) for i in range(8)]
        rr = {"i": 0, "n": 8}

        def bank():
            i = rr["i"] % rr["n"]
            rr["i"] += 1
            return pbank[i], "pb%d" % i

        h = sb("h", [128, NCH, NT], F32)
        Sst = sb("Sst", [128, 3, 4, 256], F32)
        Sbf = sb("Sbf", [128, 3, 4, 256], BF16)
        ident = sb("ident", [128, 128], F32)
        ones_b = sb("ones_b", [128, 128], BF16); ones_row = sb("ones_row", [1, 128], BF16)
        tri = sb("tri", [128, 128], F32); um = sb("um", [128, 128], F32); cm = sb("cm", [128, 128], F32)
        tris = sb("tris", [64, 64], F32); ums = sb("ums", [64, 64], F32); cms = sb("cms", [64, 64], F32)
        cols = sb("cols", [128, 10, NCH], F32)
        gn_col = sb("gn_col", [128, 2], F32)
        m16 = sb("m16", [128, 1], F32)
        useprev = sb("useprev_sb", [128, 1], F32)
        wgu = sb("wgu", [16, 512], BF16); bgate = sb("bgate", [1, 512], BF16)
        ssq_t = sb("ssq_t", [128, 2], F32); ssq2_t = sb("ssq2_t", [128, 1], F32)
        ln_t = sb("ln_t", [128, 1], F32); rs_t = sb("rs_t", [128, 1], F32)
        kc = sb("kc", [128, 4], F32)

        arena_words = (nc.sbuf_bytes_remaining - 512) // 4
        arena = sb("arena", [128, arena_words], F32)
        ar = {"off": 0}

        areg = []
        ghosts = []

        def areset():
            ar["off"] = 0

        def aalloc(shape, dt, names, parts=128):
            n = int(np.prod(shape))
            words = n if dt == F32 else (n + 1) // 2
            o = ar["off"]
            ar["off"] += words
            assert ar["off"] <= arena_words, ("arena overflow", ar["off"], arena_words)
            lo, hi = o, o + words
            inherited = []
            keepl = []
            for (l2, h2, nm2) in areg:
                if l2 < hi and lo < h2:
                    ops_ = []
                    for nm in nm2:
                        stt = S.res.pop(nm, None)
                        if stt is not None:
                            if stt["w"] is not None:
                                ops_.append(stt["w"])
                            ops_.extend(stt["r"])
                    inherited.extend(ops_)
                    if not (lo <= l2 and h2 <= hi) and ops_:
                        ghosts.append((l2, h2, ops_))
                else:
                    keepl.append((l2, h2, nm2))
            keepg = []
            for (l2, h2, ops_) in ghosts:
                if l2 < hi and lo < h2:
                    inherited.extend(ops_)
                    if lo <= l2 and h2 <= hi:
                        continue
                keepg.append((l2, h2, ops_))
            ghosts[:] = keepg
            nset = set(names)
            areg[:] = [(l2, h2, tuple(x for x in nm2 if x not in nset)) for (l2, h2, nm2) in keepl]
            areg.append((lo, hi, tuple(names)))
            for nm in names:
                S.res[nm] = {"w": None, "r": list(inherited)}
                S.known.add(nm)
            v = arena[0:parts, o:o + words]
            if dt != F32:
                v = v.bitcast(BF16)
                if n % 2:
                    v = v[:, 0:n]
            if len(shape) == 2:
                return v.rearrange("p (a b) -> p a b", a=shape[0])
            if len(shape) == 3:
                return v.rearrange("p (a b c) -> p a b c", a=shape[0], b=shape[1])
            return v

        def hres(c0, W):
            return ["h:%d" % j for j in range(c0 // 256, (c0 + W - 1) // 256 + 1)]

        def ld(dst, src, key, eng="sp", writes=None):
            S.op(eng, lambda e: e.dma_start(out=dst, in_=src), writes=writes or [key], dma=key)

        ld(ident[:], c_ident[:, :], "ident")
        ld(tri[:], c_tri[:, :], "tri"); ld(um[:], c_u[:, :], "um"); ld(cm[:], c_cm[:, :], "cm")
        ld(tris[:], c_tris[:, :], "tris"); ld(ums[:], c_us[:, :], "ums"); ld(cms[:], c_cms[:, :], "cms")
        ld(useprev[:], useprev_in[:, :], "useprev")
        ld(wgu[:], w_ggu[:, :], "wgu", eng="pool")
        ld(bgate[:], b_gg.rearrange("(o n) -> o n", o=1), "bgate", eng="pool")
        with nc.allow_non_contiguous_dma(reason="tiny per-feature vectors to per-partition columns"):
            vecs = [(0, norm_ffn[0]), (1, norm_ffn[1]), (2, norm_ple[0]), (3, norm_ple[1]), (4, norm_mix[1]),
                    (5, b_pool), (6, pool_scale)]
            for i, v in vecs:
                S.op("sp", (lambda i, v: lambda e: e.dma_start(out=cols[:, i, :], in_=v.rearrange("(c p) -> p c", p=128), allow_slow_non_contiguous=True))(i, v),
                     writes=["cols%d" % i], dma="cols%d" % i)
            S.op("sp", lambda e: e.dma_start(out=gn_col[:], in_=gla_norm.rearrange("(c p) -> p c", p=128), allow_slow_non_contiguous=True),
                 writes=["gn_col"], dma="gn_col")
        S.op("dve", lambda e: e.memset(ones_b[:], 1.0), writes=["ones_b"])
        S.op("dve", lambda e: e.memset(ones_row[:], 1.0), writes=["ones_row"])
        S.op("dve", lambda e: e.memset(m16[:], -1.0 / 16.0), writes=["m16"])
        S.op("dve", lambda e: e.memset(kc[:, 0:1], 1.0 / D), writes=["kc"])
        S.op("dve", lambda e: e.memset(kc[:, 1:2], -0.5), writes=["kc"])
        S.op("dve", lambda e: e.memset(kc[:, 2:3], 1.0 / 256.0), writes=["kc"])
        S.op("dve", lambda e: e.tensor_tensor(out=cols[:, 7, :], in0=cols[:, 5, :], in1=cols[:, 6, :], op=ALU.mult),
             reads=["cols5", "cols6"], writes=["cols7"])
        S.fence()

        def rstd_from_ss(ss_ap, out_ap, n, tag, reads, writes, tmp_ap):
            rows = ss_ap.shape[0]
            ki = 0 if n == float(D) else 2
            S.op("act", lambda e: e.activation(out=tmp_ap, in_=ss_ap, func=AF.Ln, scale=kc[0:rows, ki:ki + 1], bias=EPS),
                 reads=reads, writes=[tag + "_ln"])
            S.op("act", lambda e: e.activation(out=out_ap, in_=tmp_ap, func=AF.Exp, scale=kc[0:rows, 1:2]),
                 reads=[tag + "_ln"], writes=writes)

        def norm_fm(xn, gidx, c0, W, scr, tag, xname=None):
            sq, lnv, rs = scr
            S.op("act", lambda e: e.activation(out=sq[:, :, 0:W], in_=h[:, :, c0:c0 + W], func=AF.Square),
                 reads=hres(c0, W), writes=[tag + "sq"])
            pb, pn = bank()
            for c in range(NCH):
                S.op("pe", (lambda c: lambda e: e.matmul(pb[:, 0:W], lhsT=ones_b[:], rhs=sq[:, c, 0:W],
                                                          start=(c == 0), stop=(c == NCH - 1)))(c),
                     reads=[tag + "sq"], writes=[pn])
            rstd_from_ss(pb[:, 0:W], rs[:, 0:W], float(D), tag, [pn], [tag + "rs"], lnv[:, 0:W])
            for c in range(NCH):
                S.op("dve", (lambda c: lambda e: e.scalar_tensor_tensor(
                    out=xn[:, c, 0:W], in0=h[:, c, c0:c0 + W], scalar=cols[:, gidx, c:c + 1], in1=rs[:, 0:W],
                    op0=ALU.mult, op1=ALU.mult))(c),
                    reads=hres(c0, W) + [tag + "rs"], writes=[xname or (tag + "xn")])

        def l0_mixer(x_tok, x_prev, b0_dram, ntok, with_sample, emit_state):
            areset()
            xt = aalloc([2, D], F32, ["xt0", "xt1"])
            xnb = aalloc([2, D], BF16, ["xnb0", "xnb1"])
            junk = aalloc([D], BF16, ["junk"])
            gbc = aalloc([D], F32, ["gbc"])
            xnf = aalloc([D], F32, ["xnf"])
            dT = aalloc([NCH, ntok], BF16, ["dT:%d" % i for i in range(5)])
            wp = aalloc([4, 2, 256], BF16, ["wp"])
            band = aalloc([4, 128], BF16, ["band"]); halo = aalloc([4, 128], BF16, ["halo"]); b0 = aalloc([4, 128], BF16, ["b0"])
            bands = aalloc([4, 64], BF16, ["bands"], parts=64); sel = aalloc([64], F32, ["sel"], parts=64)
            ld(band, c_band.rearrange("w s t -> s w t"), "band", eng="pool")
            ld(halo, c_halo.rearrange("w s t -> s w t"), "halo", eng="pool")
            ld(b0, b0_dram.rearrange("w s t -> s w t"), "b0", eng="pool")
            ld(bands, c_bands.rearrange("w s t -> s w t"), "bands", eng="pool")
            ld(sel, c_sel[:, :], "sel")
            ld(gbc, norm_mix[0].partition_broadcast(128), "gbc")
            S.op("pool", lambda e: e.dma_start(out=wp, in_=w_pool.rearrange("g (k p) n -> p g k n", p=128)),
                 writes=["wp"], dma="wp")
            ntile = SEG // 128

            ntc = {"i": 0}

            def norm_tile(src_ap, slot, xslot, rows, last):
                ci = ntc["i"] % 2
                ntc["i"] += 1
                sn = "ssq%d" % ci
                S.op("sp", lambda e: e.dma_start(out=xt[0:rows, xslot, :], in_=src_ap), writes=["xt%d" % xslot], dma="xt%d" % xslot)
                S.op("pool", lambda e: e.memset(ssq_t[:, ci:ci + 1], 0.0), writes=[sn])
                S.op("act", lambda e: e.activation(out=junk[0:rows, :], in_=xt[0:rows, xslot, :], func=AF.Square,
                                                   accum_out=ssq_t[0:rows, ci:ci + 1]),
                     reads=["xt%d" % xslot], writes=[sn, "junk"])
                rstd_from_ss(ssq_t[0:rows, ci:ci + 1], rs_t[0:rows, 0:1], float(D), "l0n", [sn], ["l0rs"], ln_t[0:rows, 0:1])
                S.op("dve", lambda e: e.scalar_tensor_tensor(out=xnb[0:rows, slot, :], in0=xt[0:rows, xslot, :],
                                                             scalar=rs_t[0:rows, 0:1], in1=gbc[0:rows, :], op0=ALU.mult, op1=ALU.mult),
                     reads=["xt%d" % xslot, "l0rs", "gbc"], writes=["xnb%d" % slot])
                if last:
                    S.op("dve", lambda e: e.scalar_tensor_tensor(out=xnf[0:rows, :], in0=xt[0:rows, xslot, :],
                                                                 scalar=rs_t[0:rows, 0:1], in1=gbc[0:rows, :], op0=ALU.mult, op1=ALU.mult),
                         reads=["xt%d" % xslot, "l0rs", "gbc"], writes=["xnf"])

            norm_tile(x_prev[:, :], 1, 1, 128, False)
            prev_slot = 1
            for i in range(ntile):
                slot = i % 2
                xslot = i % 2
                last = (i == ntile - 1) and emit_state
                norm_tile(x_tok[i * 128:(i + 1) * 128, :], slot, xslot, 128, last)
                tt = (i * 128) // 512
                for half in range(2):
                    pb, pn = bank()
                    for q in range(4):
                        c = half * 4 + q
                        S.op("pe", (lambda c, q, pb: lambda e: e.matmul(pb[:, q * 128:(q + 1) * 128], lhsT=xt[:, xslot, c * 128:(c + 1) * 128],
                                                                        rhs=ident[:], start=True, stop=True))(c, q, pb),
                             reads=["xt%d" % xslot, "ident"], writes=[pn])
                    for q in range(4):
                        c = half * 4 + q
                        S.op("act", (lambda c, q, pb: lambda e: e.activation(out=h[:, c, i * 128:(i + 1) * 128], in_=pb[:, q * 128:(q + 1) * 128],
                                                                             func=AF.Identity, bias=cols[:, 7, c:c + 1]))(c, q, pb),
                             reads=[pn, "cols7"], writes=hres(i * 128, 128))
                bsel = b0 if i == 0 else band
                for half in range(2):
                    pb, pn = bank()
                    for q in range(4):
                        c = half * 4 + q
                        w = c // 2
                        S.op("pe", (lambda c, q, w, pb, bsel: lambda e: e.matmul(pb[:, q * 128:(q + 1) * 128], lhsT=xnb[:, slot, c * 128:(c + 1) * 128],
                                                                                 rhs=bsel[:, w, :], start=True, stop=False))(c, q, w, pb, bsel),
                             reads=["xnb%d" % slot, "band", "b0"], writes=[pn])
                        S.op("pe", (lambda c, q, w, pb, ps_: lambda e: e.matmul(pb[:, q * 128:(q + 1) * 128], lhsT=xnb[64:128, ps_, c * 128:(c + 1) * 128],
                                                                                rhs=halo[64:128, w, :], start=False, stop=True))(c, q, w, pb, prev_slot),
                             reads=["xnb%d" % prev_slot, "halo"], writes=[pn])
                    S.op("dve", (lambda half, pb: lambda e: e.tensor_copy(out=dT[:, half * 4:(half + 1) * 4, i * 128:(i + 1) * 128],
                                                                          in_=pb[:].rearrange("p (a b) -> p a b", a=4)))(half, pb),
                         reads=[pn], writes=["dT:%d" % tt])
                if last:
                    S.op("sp", lambda e: e.dma_start(out=ps_main[:, :], in_=xnf[113:128, :]), reads=["xnf"], dma="o_psm")
                prev_slot = slot
            if with_sample:
                A = aalloc([D], F32, ["A"], parts=64)
                Hh = aalloc([D], F32, ["Hh"], parts=64)
                xsf = aalloc([D], F32, ["xsf"], parts=64)
                xsb = aalloc([D], BF16, ["xsb"], parts=64)
                S.op("dve", lambda e: e.memset(A, 0.0), writes=["A"])
                S.op("dve", lambda e: e.memset(Hh, 0.0), writes=["Hh"])
                for j in range(2):
                    S.op("sp", (lambda j: lambda e: e.dma_start(out=A[32 * j + 15:32 * j + 31, :], in_=xs_in[16 * j:16 * j + 16, :]))(j),
                         reads=[], writes=["A"], dma="A%d" % j)
                    S.op("sp", (lambda j: lambda e: e.dma_start(out=Hh[32 * j:32 * j + 15, :], in_=hist_in[15 * j:15 * j + 15, :]))(j),
                         reads=[], writes=["Hh"], dma="H%d" % j)
                S.op("dve", lambda e: e.memset(ssq_t[:, 0:1], 0.0), writes=["ssq0"])
                S.op("act", lambda e: e.activation(out=junk[0:64, :], in_=A, func=AF.Square, accum_out=ssq_t[0:64, 0:1]),
                     reads=["A"], writes=["ssq0", "junk"])
                rstd_from_ss(ssq_t[0:64, 0:1], rs_t[0:64, 0:1], float(D), "l0n", ["ssq0"], ["l0rs"], ln_t[0:64, 0:1])
                S.op("dve", lambda e: e.scalar_tensor_tensor(out=xsf, in0=A, scalar=rs_t[0:64, 0:1], in1=gbc[0:64, :],
                                                             op0=ALU.mult, op1=ALU.mult), reads=["A", "l0rs", "gbc"], writes=["xsf"])
                S.op("dve", lambda e: e.tensor_tensor(out=xsf, in0=xsf, in1=Hh, op=ALU.add), reads=["xsf", "Hh"], writes=["xsf"])
                S.op("dve", lambda e: e.tensor_copy(out=xsb, in_=xsf), reads=["xsf"], writes=["xsb"])
                for j in range(2):
                    S.op("sp", (lambda j: lambda e: e.dma_start(out=ps_smp[j, :, :], in_=xsf[32 * j + 16:32 * j + 31, :]))(j),
                         reads=["xsf"], dma="o_pss%d" % j)
                for half in range(2):
                    pb, pn = bank()
                    for q in range(4):
                        c = half * 4 + q
                        S.op("pe", (lambda c, q, pb: lambda e: e.matmul(pb[:, q * 64:(q + 1) * 64], lhsT=A[:, c * 128:(c + 1) * 128],
                                                                        rhs=sel, start=True, stop=True))(c, q, pb),
                             reads=["A", "sel"], writes=[pn])
                    for q in range(4):
                        c = half * 4 + q
                        S.op("act", (lambda c, q, pb: lambda e: e.activation(out=h[:, c, SEG:NT], in_=pb[:, q * 64:(q + 1) * 64],
                                                                             func=AF.Identity, bias=cols[:, 7, c:c + 1]))(c, q, pb),
                             reads=[pn, "cols7"], writes=hres(SEG, 64))
                for half in range(2):
                    pb, pn = bank()
                    for q in range(4):
                        c = half * 4 + q
                        w = c // 2
                        S.op("pe", (lambda c, q, w, pb: lambda e: e.matmul(pb[:, q * 64:(q + 1) * 64], lhsT=xsb[:, c * 128:(c + 1) * 128],
                                                                           rhs=bands[:, w, :], start=True, stop=True))(c, q, w, pb),
                             reads=["xsb", "bands"], writes=[pn])
                    S.op("dve", (lambda half, pb: lambda e: e.tensor_copy(out=dT[:, half * 4:(half + 1) * 4, SEG:NT],
                                                                          in_=pb[:, 0:256].rearrange("p (a b) -> p a b", a=4)))(half, pb),
                         reads=[pn], writes=["dT:4"])
            for g in range(4):
                for mc in range(2):
                    c = 2 * g + mc
                    for (c0, W) in _tiles(ntok):
                        tt = c0 // 512
                        pb, pn = bank()
                        for kc in range(2):
                            S.op("pe", (lambda g, mc, kc, pb, c0, W: lambda e: e.matmul(
                                pb[:, 0:W], lhsT=wp[:, g, kc, mc * 128:(mc + 1) * 128], rhs=dT[:, 2 * g + kc, c0:c0 + W],
                                start=(kc == 0), stop=(kc == 1)))(g, mc, kc, pb, c0, W),
                                reads=["wp", "dT:%d" % tt], writes=[pn])
                        S.op("dve", (lambda c, pb, c0, W: lambda e: e.scalar_tensor_tensor(
                            out=h[:, c, c0:c0 + W], in0=pb[:, 0:W], scalar=cols[:, 6, c:c + 1], in1=h[:, c, c0:c0 + W],
                            op0=ALU.mult, op1=ALU.add))(c, pb, c0, W),
                            reads=[pn] + hres(c0, W), writes=hres(c0, W))

        def ffn(layer, ntok, hook=None):
            areset()
            xn = aalloc([NCH, ntok], BF16, ["fnxn"])
            FS = max(SLABS)
            wg = [aalloc([NCH, FS * 128], BF16, ["wg%d" % i]) for i in range(2)]
            wu = [aalloc([NCH, FS * 128], BF16, ["wu%d" % i]) for i in range(2)]
            wd = [aalloc([FS, D], BF16, ["wd%d" % i]) for i in range(2)]
            mark = ar["off"]
            sq = aalloc([NCH, 512], BF16, ["fnsq"]); lnv = aalloc([512], F32, ["fn_ln"]); rs = aalloc([512], F32, ["fnrs"])

            def load_slab(s, f0):
                fs = SLABS[s]
                b = s % 2
                S.op("pool", lambda e: e.dma_start(out=wg[b][:, :, 0:fs * 128],
                                                   in_=w_fg[layer].rearrange("(c p) f -> p c f", p=128)[:, :, f0 * 128:(f0 + fs) * 128]),
                     writes=["wg%d" % b], dma="wg%d" % b)
                S.op("pool", lambda e: e.dma_start(out=wu[b][:, :, 0:fs * 128],
                                                   in_=w_fu[layer].rearrange("(c p) f -> p c f", p=128)[:, :, f0 * 128:(f0 + fs) * 128]),
                     writes=["wu%d" % b], dma="wu%d" % b)
                S.op("pool", lambda e: e.dma_start(out=wd[b][:, 0:fs, :],
                                                   in_=w_fd[layer, f0 * 128:(f0 + fs) * 128, :].rearrange("(f p) n -> p f n", p=128)),
                     writes=["wd%d" % b], dma="wd%d" % b)

            f0s = [sum(SLABS[:s]) for s in range(len(SLABS))]
            load_slab(0, f0s[0])
            load_slab(1, f0s[1])
            hooked = hook() if hook is not None else None
            for (c0, W) in _tiles(ntok):
                norm_fm(xn[:, :, c0:c0 + W], layer, c0, W, (sq, lnv, rs), "fn")
            ar["off"] = mark
            act = [aalloc([FS, 512], BF16, ["act%d" % i]) for i in range(2)]
            sg = [aalloc([512], F32, ["sg%d" % i]) for i in range(2)]
            it = 0
            for s, fs in enumerate(SLABS):
                b = s % 2
                for (c0, W) in _tiles(ntok):
                    tt = c0 // 512
                    ab = it % 2
                    it += 1
                    for f in range(fs):
                        pg, png = bank()
                        pu, pnu = bank()
                        for k in range(NCH):
                            S.op("pe", (lambda f, k, pg: lambda e: e.matmul(pg[:, 0:W], lhsT=wg[b][:, k, f * 128:(f + 1) * 128],
                                                                            rhs=xn[:, k, c0:c0 + W], start=(k == 0), stop=(k == NCH - 1)))(f, k, pg),
                                 reads=["wg%d" % b, "fnxn"], writes=[png])
                        for k in range(NCH):
                            S.op("pe", (lambda f, k, pu: lambda e: e.matmul(pu[:, 0:W], lhsT=wu[b][:, k, f * 128:(f + 1) * 128],
                                                                            rhs=xn[:, k, c0:c0 + W], start=(k == 0), stop=(k == NCH - 1)))(f, k, pu),
                                 reads=["wu%d" % b, "fnxn"], writes=[pnu])
                        sgi = (it + f) % 2
                        S.op("act", (lambda pg, sgi: lambda e: e.activation(out=sg[sgi][:, 0:W], in_=pg[:, 0:W], func=AF.Silu))(pg, sgi),
                             reads=[png], writes=["sg%d" % sgi])
                        S.op("dve", (lambda f, pu, sgi: lambda e: e.tensor_tensor(out=act[ab][:, f, 0:W], in0=pu[:, 0:W], in1=sg[sgi][:, 0:W],
                                                                                  op=ALU.mult))(f, pu, sgi),
                             reads=[pnu, "sg%d" % sgi], writes=["act%d" % ab])
                    for c in range(NCH):
                        po, pno = bank()
                        for f in range(fs):
                            S.op("pe", (lambda f, c, po: lambda e: e.matmul(po[:, 0:W], lhsT=wd[b][:, f, c * 128:(c + 1) * 128],
                                                                            rhs=act[ab][:, f, 0:W], start=(f == 0), stop=(f == fs - 1)))(f, c, po),
                                 reads=["wd%d" % b, "act%d" % ab], writes=[pno])
                        S.op("dve", (lambda c, po: lambda e: e.tensor_tensor(out=h[:, c, c0:c0 + W], in0=po[:, 0:W], in1=h[:, c, c0:c0 + W],
                                                                             op=ALU.add))(c, po),
                             reads=[pno] + hres(c0, W), writes=hres(c0, W))
                if s + 2 < len(SLABS):
                    load_slab(s + 2, f0s[s + 2])
            return hooked

        def aalloc_at(off_words, shape, dt, names, parts=128):
            save = ar["off"]
            ar["off"] = off_words
            v = aalloc(shape, dt, names, parts)
            ar["off"] = save
            return v

        PLE_W_OFF = arena_words - (NCH * D + 2 * D) // 2
        WIN_WORDS = NCH * GIN // 2
        WIN_OFF = 12352
        WOUT_OFF = WIN_OFF + WIN_WORDS
        GSM_OFF = WOUT_OFF + NCH * D // 2
        assert WOUT_OFF <= PLE_W_OFF

        def gla_pre():
            win = aalloc_at(WIN_OFF, [NCH, GIN], BF16, ["win"])
            S.op("pool", lambda e: e.dma_start(out=win, in_=w_gin.rearrange("(c p) n -> p c n", p=128)), writes=["win"], dma="win")
            return win

        def ple_pre(layer):
            wpg = aalloc_at(PLE_W_OFF, [NCH, D], BF16, ["wpg"])
            wpp = aalloc_at(PLE_W_OFF + NCH * D // 2, [2, D], BF16, ["wpp"])
            S.op("pool", lambda e: e.dma_start(out=wpg, in_=w_pg[layer].rearrange("(c p) n -> p c n", p=128)), writes=["wpg"], dma="wpg")
            S.op("pool", lambda e: e.dma_start(out=wpp, in_=w_pp[layer].rearrange("(c p) n -> p c n", p=128)), writes=["wpp"], dma="wpp")
            return wpg, wpp

        def zipper(*gens):
            alive = list(gens)
            while alive:
                for g in list(alive):
                    try:
                        next(g)
                    except StopIteration:
                        alive.remove(g)

        def drain(g):
            for _ in g:
                pass

        def ple(layer, p_tok, p_s, ntok, wts, hook=None):
            wpg, wpp = wts
            areset()
            xn2 = [aalloc([NCH, 512], BF16, ["pnxn%d" % i]) for i in range(2)]
            pT = aalloc([2, ntok], BF16, ["pT%d" % i for i in range(5)])
            ptb = [aalloc([2, 256], F32, ["ptb%d" % i]) for i in range(2)]
            sq = aalloc([NCH, 512], BF16, ["pnsq"]); lnv = aalloc([512], F32, ["pn_ln"]); rs = aalloc([512], F32, ["pnrs"])
            sgt = [aalloc([512], F32, ["sgt%d" % i]) for i in range(2)]
            tmp = [aalloc([512], F32, ["tmp%d" % i]) for i in range(2)]
            assert ar["off"] <= WIN_OFF, (ar["off"], WIN_OFF)
            tl = _tiles(ntok)
            hooked = hook() if hook is not None else None

            def gen_pT():
                for bb in range(SEG // 256):
                    sl = bb % 2
                    S.op("sp", lambda e: e.dma_start(out=ptb[sl], in_=p_tok[bb * 256:(bb + 1) * 256, :].rearrange("(j p) n -> p j n", p=128)),
                         writes=["ptb%d" % sl], dma="ptb%d" % sl)
                    pb, pn = bank()
                    for j in range(2):
                        for kc in range(2):
                            S.op("pe", lambda e: e.matmul(pb[:, (j * 2 + kc) * 128:(j * 2 + kc + 1) * 128], lhsT=ptb[sl][:, j, kc * 128:(kc + 1) * 128],
                                                          rhs=ident[:], start=True, stop=True), reads=["ptb%d" % sl, "ident"], writes=[pn])
                    for j in range(2):
                        col = bb * 256 + j * 128
                        S.op("act", lambda e: e.activation(out=pT[:, :, col:col + 128], in_=pb[:, j * 256:(j + 1) * 256].rearrange("p (a b) -> p a b", a=2),
                                                           func=AF.Copy), reads=[pn], writes=["pT%d" % (bb // 2)])
                    yield
                if p_s is not None:
                    pts = ptb[0][0:64, 0, :]
                    S.op("dve", lambda e: e.memset(pts, 0.0), writes=["ptb0"])
                    for j in range(2):
                        S.op("sp", lambda e: e.dma_start(out=ptb[0][32 * j:32 * j + 16, 0, :], in_=p_s[16 * j:16 * j + 16, :]),
                             writes=["ptb0"], dma="pts%d" % j)
                    pb, pn = bank()
                    for kc in range(2):
                        S.op("pe", lambda e: e.matmul(pb[:, kc * 64:(kc + 1) * 64], lhsT=pts[:, kc * 128:(kc + 1) * 128],
                                                      rhs=ident[0:64, 0:64], start=True, stop=True), reads=["ptb0", "ident"], writes=[pn])
                    S.op("act", lambda e: e.activation(out=pT[:, :, SEG:NT], in_=pb[:, 0:128].rearrange("p (a b) -> p a b", a=2), func=AF.Copy),
                         reads=[pn], writes=["pT4"])
                    yield

            def gen_norm(t):
                c0, W = tl[t]
                norm_fm(xn2[t % 2][:, :, 0:W], 2 + layer, c0, W, (sq, lnv, rs), "pn", xname="pnxn%d" % (t % 2))
                yield

            def gen_work(t):
                c0, W = tl[t]
                xs_ = xn2[t % 2]
                xr = "pnxn%d" % (t % 2)
                for c in range(NCH):
                    sl = c % 2
                    pg, png = bank()
                    pp, pnp = bank()
                    for k in range(NCH):
                        S.op("pe", lambda e: e.matmul(pg[:, 0:W], lhsT=wpg[:, k, c * 128:(c + 1) * 128], rhs=xs_[:, k, 0:W],
                                                      start=(k == 0), stop=(k == NCH - 1)), reads=["wpg", xr], writes=[png])
                    for k in range(2):
                        S.op("pe", lambda e: e.matmul(pp[:, 0:W], lhsT=wpp[:, k, c * 128:(c + 1) * 128], rhs=pT[:, k, c0:c0 + W],
                                                      start=(k == 0), stop=(k == 1)), reads=["wpp", "pT%d" % t], writes=[pnp])
                    S.op("act", lambda e: e.activation(out=sgt[sl][:, 0:W], in_=pg[:, 0:W], func=AF.Sigmoid), reads=[png], writes=["sgt%d" % sl])
                    S.op("dve", lambda e: e.tensor_tensor(out=tmp[sl][:, 0:W], in0=pp[:, 0:W], in1=sgt[sl][:, 0:W], op=ALU.mult),
                         reads=[pnp, "sgt%d" % sl], writes=["tmp%d" % sl])
                    S.op("dve", lambda e: e.tensor_tensor(out=h[:, c, c0:c0 + W], in0=h[:, c, c0:c0 + W], in1=tmp[sl][:, 0:W], op=ALU.add),
                         reads=["tmp%d" % sl] + hres(c0, W), writes=hres(c0, W))
                    yield

            gp = gen_pT()
            next(gp); next(gp)
            drain(gen_norm(0))

            def rest():
                for t in range(len(tl)):
                    if t + 1 < len(tl):
                        yield from gen_norm(t + 1)
                    yield from gen_work(t)

            zipper(gp, rest())
            return hooked

        def gla(ntok, full, win):
            areset()
            if full:
                wout = aalloc_at(WOUT_OFF, [NCH, D], BF16, ["wout"])
            xn = [aalloc([NCH, GT], BF16, ["gxn%d" % i]) for i in range(2)]
            lnv = aalloc([GT], F32, ["gn_ln"]); rs = aalloc([GT], F32, ["gnrs"])
            grT = aalloc([GT], BF16, ["grT"])
            mk = ar["off"]
            sq = aalloc([NCH, GT], BF16, ["gsq", "la", "e3"])
            la = arena[:, mk:mk + 512]
            e3 = arena[:, mk + 512:mk + 1024]
            kend = [aalloc([2, 512], BF16, ["kend%d_%d" % (i, j) for j in range(2)]) for i in range(2)]
            vtm = [aalloc([2, D], BF16, ["vtm%d_%d" % (i, j) for j in range(2)]) for i in range(2)]
            dec = [aalloc([8], F32, ["dec%d" % i]) for i in range(2)]
            if full:
                e1 = aalloc([GT], F32, ["e1_0", "e1_1"]); e2 = aalloc([GT], F32, ["e2_0", "e2_1"])
                qdec = [aalloc([4, GT], BF16, ["qdec%d_%d" % (i, j) for j in range(4)]) for i in range(2)]
                kinv = [aalloc([4, GT], BF16, ["kinv%d_%d" % (i, j) for j in range(4)]) for i in range(2)]
                sc = [aalloc([2, 128], BF16, ["sc%d_%d" % (i, j) for j in range(2)]) for i in range(2)]
                sgg = aalloc([NCH, GT], BF16, ["sgg%d" % i for i in range(8)])
                og = aalloc([NCH, GT], BF16, ["og%d" % i for i in range(8)])
                osq = aalloc([4, GT], BF16, ["osq0", "osq1"])
                save_ = ar["off"]
                ar["off"] = GSM_OFF
                rso = [aalloc([GT], F32, ["rso%d" % i]) for i in range(2)]
                lno = [aalloc([GT], F32, ["go_ln"])] * 2
                ogt = [aalloc([GT], F32, ["ogt"])] * 2
                ar["off"] = save_
            assert ar["off"] <= WIN_OFF, (ar["off"], WIN_OFF)
            if full:
                S.op("pool", lambda e: e.dma_start(out=wout, in_=w_gout.rearrange("(c p) n -> p c n", p=128)), writes=["wout"], dma="wout")
            QO, KO, VO, GO, RO = 0, 512, 1024, 2048, 3072
            tiles = [(c0, GT, "P") for c0 in range(0, SEG, GT)]
            if ntok > SEG:
                tiles.append((SEG, 64, "S"))
            rp = {"i": 0}
            rq = {"i": 0}

            def bankP():
                i = rp["i"] % 3
                rp["i"] += 1
                return pbank[i], "pb%d" % i

            def bankR():
                i = 6 + rq["i"] % 2
                rq["i"] += 1
                return pbank[i], "pb%d" % i

            def blocks_of(W, kind):
                if kind == "P":
                    return [(bi * 128, 128, [(0, 128, 0)], tri, um, cm) for bi in range(W // 128)]
                return [(0, 64, [(0, 16, 1), (32, 16, 2)], tris, ums, cms)]

            def Pn(t):
                c0, W, kind = tiles[t]
                s = t % 2
                xs_ = xn[s % len(xn)]
                hr = hres(c0, W)
                S.op("act", lambda e: e.activation(out=sq[:, :, 0:W], in_=h[:, :, c0:c0 + W], func=AF.Square), reads=hr, writes=["gsq", "la", "e3"])
                pb, pn = bankP()
                for c in range(NCH):
                    S.op("pe", lambda e: e.matmul(pb[:, 0:W], lhsT=ones_b[:], rhs=sq[:, c, 0:W], start=(c == 0), stop=(c == NCH - 1)),
                         reads=["gsq", "la", "e3"], writes=[pn])
                rstd_from_ss(pb[:, 0:W], rs[:, 0:W], float(D), "gn", [pn], ["gnrs"], lnv[:, 0:W])
                for c in range(NCH):
                    S.op("dve", lambda e: e.scalar_tensor_tensor(out=xs_[:, c, 0:W], in0=h[:, c, c0:c0 + W], scalar=cols[:, 4, c:c + 1],
                                                                 in1=rs[:, 0:W], op0=ALU.mult, op1=ALU.mult),
                         reads=hr + ["gnrs"], writes=["gxn%d" % (s % len(xn))])
                yield

            def P(t, with_norm=True):
                c0, W, kind = tiles[t]
                s = t % 2
                xs_ = xn[s % len(xn)]
                hr = hres(c0, W)
                blocks = blocks_of(W, kind)
                if with_norm:
                    yield from Pn(t)
                xr = "gxn%d" % (s % len(xn))
                yield
                pb, pn = bankP()
                for k in range(NCH):
                    S.op("pe", lambda e: e.matmul(pb[0:16, 0:W], lhsT=win[:, k, RO:RO + 16], rhs=xs_[:, k, 0:W], start=(k == 0), stop=(k == NCH - 1)),
                         reads=["win", xr], writes=[pn])
                S.op("act", lambda e: e.activation(out=grT[0:16, 0:W], in_=pb[0:16, 0:W], func=AF.Copy), reads=[pn], writes=["grT"])
                pdec, pndec = pbank[3], "pb3"
                yield
                ndec = 0
                for bi, (b0, BW, subs, trim, umm, cmm) in enumerate(blocks):
                    R = BW
                    for vh in range(2):
                        pv, pnv = bankP()
                        for k in range(NCH):
                            S.op("pe", lambda e: e.matmul(pv[0:R, :], lhsT=xs_[:, k, b0:b0 + BW], rhs=win[:, k, VO + vh * 512:VO + (vh + 1) * 512],
                                                          start=(k == 0), stop=(k == NCH - 1)), reads=["win", xr], writes=[pnv])
                        S.op("act", lambda e: e.activation(out=vtm[s][0:R, bi, vh * 512:(vh + 1) * 512], in_=pv[0:R, :], func=AF.Copy),
                             reads=[pnv], writes=["vtm%d_%d" % (s, bi)])
                        yield
                    pb, pn = bankP()
                    S.op("pe", lambda e: e.matmul(pb[0:R, :], lhsT=grT[0:16, b0:b0 + BW], rhs=wgu[:], start=True, stop=False), reads=["grT", "wgu"], writes=[pn])
                    S.op("pe", lambda e: e.matmul(pb[0:R, :], lhsT=ones_row[0:1, 0:BW], rhs=bgate[:], start=False, stop=True), reads=["bgate", "ones_row"], writes=[pn])
                    S.op("act", lambda e: e.activation(out=la[0:R, :], in_=pb[0:R, :], func=AF.Exp, scale=-1.0), reads=[pn], writes=["la"])
                    S.op("act", lambda e: e.activation(out=la[0:R, :], in_=la[0:R, :], func=AF.Ln, bias=1.0), reads=["la"], writes=["la"])
                    yield
                    pk, pnk = bankP()
                    for k in range(NCH):
                        S.op("pe", lambda e: e.matmul(pk[0:R, :], lhsT=xs_[:, k, b0:b0 + BW], rhs=win[:, k, KO:KO + 512], start=(k == 0), stop=(k == NCH - 1)),
                             reads=["win", xr], writes=[pnk])
                    pb, pn = bankP()
                    S.op("pe", lambda e: e.matmul(pb[0:R, :], lhsT=umm[0:R, 0:R], rhs=la[0:R, :], start=True, stop=True), reads=["la", "um"], writes=[pn])
                    S.op("act", lambda e: e.activation(out=e3[0:R, :], in_=pb[0:R, :], func=AF.Exp), reads=[pn], writes=["e3"])
                    S.op("dve", lambda e: e.tensor_tensor(out=kend[s][0:R, bi, :], in0=pk[0:R, :], in1=e3[0:R, :], op=ALU.mult),
                         reads=[pnk, "e3"], writes=["kend%d_%d" % (s, bi)])
                    yield
                    for (pbase, ln, sid) in subs:
                        for hd in range(4):
                            col = ndec * 4 + hd
                            S.op("pe", lambda e: e.matmul(pdec[:, col:col + 1], lhsT=la[pbase:pbase + ln, hd * 128:(hd + 1) * 128],
                                                          rhs=m16[pbase:pbase + ln, 0:1], start=True, stop=True), reads=["la", "m16"], writes=[pndec])
                        ndec += 1
                    if full:
                        for hd in range(4):
                            pbT, pnT = bankP()
                            S.op("pe", lambda e: e.matmul(pbT[:, 0:BW], lhsT=la[0:BW, hd * 128:(hd + 1) * 128], rhs=trim[0:BW, 0:BW], start=True, stop=True),
                                 reads=["la", "tri"], writes=[pnT])
                            S.op("act", lambda e: e.activation(out=e1[:, hd % 2 * 128:hd % 2 * 128 + BW], in_=pbT[:, 0:BW], func=AF.Exp), reads=[pnT], writes=["e1_%d" % (hd % 2)])
                            S.op("act", lambda e: e.activation(out=e2[:, hd % 2 * 128:hd % 2 * 128 + BW], in_=pbT[:, 0:BW], func=AF.Exp, scale=-1.0), reads=[pnT], writes=["e2_%d" % (hd % 2)])
                            pq, pnq = bankP()
                            for k in range(NCH):
                                S.op("pe", lambda e: e.matmul(pq[:, 0:BW], lhsT=win[:, k, QO + hd * 128:QO + (hd + 1) * 128], rhs=xs_[:, k, b0:b0 + BW],
                                                              start=(k == 0), stop=(k == NCH - 1)), reads=["win", xr], writes=[pnq])
                            for k in range(NCH):
                                S.op("pe", lambda e: e.matmul(pq[:, 128:128 + BW], lhsT=win[:, k, KO + hd * 128:KO + (hd + 1) * 128], rhs=xs_[:, k, b0:b0 + BW],
                                                              start=(k == 0), stop=(k == NCH - 1)), reads=["win", xr], writes=[pnq])
                            S.op("dve", lambda e: e.scalar_tensor_tensor(out=qdec[s][:, hd, b0:b0 + BW], in0=pq[:, 0:BW], scalar=128.0 ** -0.5,
                                                                         in1=e1[:, hd % 2 * 128:hd % 2 * 128 + BW], op0=ALU.mult, op1=ALU.mult),
                                 reads=[pnq, "e1_%d" % (hd % 2)], writes=["qdec%d_%d" % (s, hd)])
                            S.op("dve", lambda e: e.tensor_tensor(out=kinv[s][:, hd, b0:b0 + BW], in0=pq[:, 128:128 + BW], in1=e2[:, hd % 2 * 128:hd % 2 * 128 + BW], op=ALU.mult),
                                 reads=[pnq, "e2_%d" % (hd % 2)], writes=["kinv%d_%d" % (s, hd)])
                            yield
                S.op("act", lambda e: e.activation(out=dec[s][:, 0:ndec * 4], in_=pdec[:, 0:ndec * 4], func=AF.Exp), reads=[pndec], writes=["dec%d" % s])

            def Rr(t):
                c0, W, kind = tiles[t]
                s = t % 2
                xs_ = xn[s % len(xn)]
                xr = "gxn%d" % (s % len(xn))
                hr = hres(c0, W)
                blocks = blocks_of(W, kind)
                if full:
                    for c in range(NCH):
                        pg, png = bankR()
                        for k in range(NCH):
                            S.op("pe", lambda e: e.matmul(pg[:, 0:W], lhsT=win[:, k, GO + c * 128:GO + (c + 1) * 128], rhs=xs_[:, k, 0:W],
                                                          start=(k == 0), stop=(k == NCH - 1)), reads=["win", xr], writes=[png])
                        S.op("act", lambda e: e.activation(out=sgg[:, c, 0:W], in_=pg[:, 0:W], func=AF.Silu), reads=[png], writes=["sgg%d" % c])
                    yield
                for pair in range(2):
                    heads = (2 * pair, 2 * pair + 1)
                    po = {}
                    if full:
                        for hh_, hd in enumerate(heads):
                            po[hd] = (pbank[4 + hh_], "pb%d" % (4 + hh_))
                    di = 0
                    def scm(bi_):
                        b0_, BW_, subs_, trim_, umm_, cmm_ = blocks[bi_]
                        psc, pnsc = bankR()
                        si_ = (pair * 2 + bi_) % 2
                        for hh, hd in enumerate(heads):
                            S.op("pe", lambda e: e.matmul(psc[0:BW_, hh * 128:hh * 128 + BW_], lhsT=kinv[s][:, hd, b0_:b0_ + BW_], rhs=qdec[s][:, hd, b0_:b0_ + BW_],
                                                          start=True, stop=True), reads=["kinv%d_%d" % (s, hd), "qdec%d_%d" % (s, hd)], writes=[pnsc])
                        for hh, hd in enumerate(heads):
                            S.op("dve", lambda e: e.tensor_tensor(out=sc[si_][0:BW_, hh, 0:BW_], in0=psc[0:BW_, hh * 128:hh * 128 + BW_], in1=cmm_[0:BW_, 0:BW_], op=ALU.mult),
                                 reads=[pnsc, "cm"], writes=["sc%d_%d" % (si_, hh)])

                    if full:
                        scm(0)
                        yield
                    for bi, (b0, BW, subs, trim, umm, cmm) in enumerate(blocks):
                        if full:
                            si = (pair * 2 + bi) % 2
                            if bi + 1 < len(blocks):
                                scm(bi + 1)
                            for hh, hd in enumerate(heads):
                                pob, pno = po[hd]
                                for dvc in range(2):
                                    oc = dvc * GT + b0
                                    S.op("pe", lambda e: e.matmul(pob[:, oc:oc + BW], lhsT=vtm[s][0:BW, bi, hd * 256 + dvc * 128:hd * 256 + (dvc + 1) * 128],
                                                                  rhs=sc[si][0:BW, hh, 0:BW], start=True, stop=False, skip_group_check=True),
                                         reads=["vtm%d_%d" % (s, bi), "sc%d_%d" % (si, hh)], writes=[pno])
                                    for sj, (pbase, ln, sid) in enumerate(subs):
                                        slot = sid
                                        lastsub = sj == len(subs) - 1
                                        S.op("pe", lambda e: e.matmul(pob[:, oc + pbase:oc + pbase + ln], lhsT=Sbf[:, slot, hd, dvc * 128:(dvc + 1) * 128],
                                                                      rhs=qdec[s][:, hd, b0 + pbase:b0 + pbase + ln], start=False, stop=lastsub, skip_group_check=True),
                                             reads=["Sbf%d_%d" % (slot, hd), "qdec%d_%d" % (s, hd)], writes=[pno])
                        yield
                        for sj, (pbase, ln, sid) in enumerate(subs):
                            pu, pnu = bankR()
                            for hh, hd in enumerate(heads):
                                S.op("pe", lambda e: e.matmul(pu[:, hh * 256:(hh + 1) * 256], lhsT=kend[s][pbase:pbase + ln, bi, hd * 128:(hd + 1) * 128],
                                                              rhs=vtm[s][pbase:pbase + ln, bi, hd * 256:(hd + 1) * 256], start=True, stop=True),
                                     reads=["kend%d_%d" % (s, bi), "vtm%d_%d" % (s, bi)], writes=[pnu])
                            for hh, hd in enumerate(heads):
                                dcol = (di + sj) * 4 + hd
                                S.op("dve", lambda e: e.scalar_tensor_tensor(out=Sst[:, sid, hd, :], in0=Sst[:, sid, hd, :], scalar=dec[s][:, dcol:dcol + 1],
                                                                             in1=pu[:, hh * 256:(hh + 1) * 256], op0=ALU.mult, op1=ALU.add),
                                     reads=[pnu, "dec%d" % s, "Sst%d_%d" % (sid, hd)], writes=["Sst%d_%d" % (sid, hd)])
                                if full and sid == 0:
                                    S.op("act", lambda e: e.activation(out=Sbf[:, 0, hd, :], in_=Sst[:, 0, hd, :], func=AF.Copy),
                                         reads=["Sst0_%d" % hd], writes=["Sbf0_%d" % hd])
                        di += len(subs)
                        yield
                    if full:
                        for hh, hd in enumerate(heads):
                            pob, pno = po[hd]
                            S.op("act", lambda e: e.activation(out=osq[:, hh * 2:hh * 2 + 2, 0:W], in_=pob[:].rearrange("p (a b) -> p a b", a=2)[:, :, 0:W], func=AF.Square),
                                 reads=[pno], writes=["osq%d" % hh])
                            pb, pn = bankR()
                            for dvc in range(2):
                                S.op("pe", lambda e: e.matmul(pb[:, 0:W], lhsT=ones_b[:], rhs=osq[:, hh * 2 + dvc, 0:W], start=(dvc == 0), stop=(dvc == 1)),
                                     reads=["osq%d" % hh, "ones_b"], writes=[pn])
                            rstd_from_ss(pb[:, 0:W], rso[hh][:, 0:W], 256.0, "go", [pn], ["rso%d" % hh], lno[hh][:, 0:W])
                            yield
                        for hh, hd in enumerate(heads):
                            pob, pno = po[hd]
                            for dvc in range(2):
                                c = hd * 2 + dvc
                                ti = dvc
                                S.op("dve", lambda e: e.scalar_tensor_tensor(out=ogt[ti][:, 0:W], in0=pob[:, dvc * GT:dvc * GT + W], scalar=gn_col[:, dvc:dvc + 1],
                                                                             in1=rso[hh][:, 0:W], op0=ALU.mult, op1=ALU.mult),
                                     reads=[pno, "rso%d" % hh], writes=["ogt"])
                                S.op("dve", lambda e: e.tensor_tensor(out=og[:, c, 0:W], in0=ogt[ti][:, 0:W], in1=sgg[:, c, 0:W], op=ALU.mult),
                                     reads=["ogt", "sgg%d" % c], writes=["og%d" % c])
                        yield
                if full:
                    for c in range(NCH):
                        pb, pn = bankR()
                        for k in range(NCH):
                            S.op("pe", lambda e: e.matmul(pb[:, 0:W], lhsT=wout[:, k, c * 128:(c + 1) * 128], rhs=og[:, k, 0:W], start=(k == 0), stop=(k == NCH - 1)),
                                 reads=["wout", "og%d" % k], writes=[pn])
                        S.op("dve", lambda e: e.tensor_tensor(out=h[:, c, c0:c0 + W], in0=pb[:, 0:W], in1=h[:, c, c0:c0 + W], op=ALU.add),
                             reads=[pn] + hr, writes=hr)
                        if c % 2 == 1:
                            yield

            drain(Pn(0))
            drain(P(0, False))
            if len(tiles) > 1:
                drain(Pn(1))
            for t in range(len(tiles)):
                gens = [Rr(t)]
                if t + 1 < len(tiles):
                    gens.append(P(t + 1, False))
                if t + 2 < len(tiles):
                    gens.append(Pn(t + 2))
                zipper(*gens)

        def final(ntok):
            areset()
            gbc = aalloc([D], F32, ["gfin"])
            yt = [aalloc([D], F32, ["yt%d" % i]) for i in range(2)]
            junk = aalloc([D], BF16, ["junkf"])
            ld(gbc, norm_final.partition_broadcast(128), "gfin")
            tl = [(i * 128, 128) for i in range(SEG // 128)] + ([(SEG, 64)] if ntok > SEG else [])
            for i, (c0, W) in enumerate(tl):
                tt = c0 // 512
                sl = i % 2
                pbs = []
                for half in range(2):
                    pb, pn = bank()
                    pbs.append((pb, pn))
                    for q in range(4):
                        c = half * 4 + q
                        S.op("pe", (lambda c, q, pb: lambda e: e.matmul(pb[0:W, q * 128:(q + 1) * 128], lhsT=h[:, c, c0:c0 + W], rhs=ident[:],
                                                                        start=True, stop=True))(c, q, pb),
                             reads=hres(c0, W) + ["ident"], writes=[pn])
                S.op("pool", lambda e: e.memset(ssq_t[:, 0:2], 0.0), writes=["ssq0", "ssq1"])
                for half in range(2):
                    pb, pn = pbs[half]
                    S.op("act", (lambda half, pb: lambda e: e.activation(out=junk[0:W, half * 512:(half + 1) * 512], in_=pb[0:W, :], func=AF.Square,
                                                                         accum_out=ssq_t[0:W, half:half + 1]))(half, pb),
                         reads=[pn, "ssq%d" % half], writes=["ssq%d" % half, "junkf"])
                S.op("dve", lambda e: e.tensor_tensor(out=ssq2_t[0:W, 0:1], in0=ssq_t[0:W, 0:1], in1=ssq_t[0:W, 1:2], op=ALU.add),
                     reads=["ssq0", "ssq1"], writes=["ssq2"])
                rstd_from_ss(ssq2_t[0:W, 0:1], rs_t[0:W, 0:1], float(D), "fin", ["ssq2"], ["finrs"], ln_t[0:W, 0:1])
                for half in range(2):
                    pb, pn = pbs[half]
                    S.op("dve", (lambda half, pb, sl: lambda e: e.scalar_tensor_tensor(out=yt[sl][0:W, half * 512:(half + 1) * 512], in0=pb[0:W, :],
                                                                                       scalar=rs_t[0:W, 0:1], in1=gbc[0:W, half * 512:(half + 1) * 512],
                                                                                       op0=ALU.mult, op1=ALU.mult))(half, pb, sl),
                         reads=[pn, "finrs", "gfin"], writes=["yt%d" % sl])
                if W == 128:
                    S.op("sp", (lambda c0, sl: lambda e: e.dma_start(out=y_main[c0:c0 + 128, :], in_=yt[sl][:, :]))(c0, sl), reads=["yt%d" % sl], dma="o_y%d" % sl)
                else:
                    for j in range(2):
                        S.op("sp", (lambda j, sl: lambda e: e.dma_start(out=y_smp[16 * j:16 * j + 16, :], in_=yt[sl][32 * j:32 * j + 16, :]))(j, sl),
                             reads=["yt%d" % sl], dma="o_ys%d" % j)

        stages = []
        S.op("dve", lambda e: e.memset(Sst[:, 0, :, :], 0.0), writes=["Sst0_%d" % i for i in range(4)])
        l0_mixer(x_warm, x_wprev, c_band0w, SEG, False, False)
        wts = ffn(0, SEG, lambda: ple_pre(0))
        win_ = ple(0, p_warm, None, SEG, wts, gla_pre)
        gla(SEG, False, win_)
        for hd in range(4):
            S.op("dve", (lambda hd: lambda e: e.tensor_scalar_mul(out=Sst[:, 0, hd, :], in0=Sst[:, 0, hd, :], scalar1=useprev[:, 0:1]))(hd),
                 reads=["Sst0_%d" % hd, "useprev"], writes=["Sst0_%d" % hd])
            S.op("act", (lambda hd: lambda e: e.activation(out=Sbf[:, 0, hd, :], in_=Sst[:, 0, hd, :], func=AF.Copy))(hd),
                 reads=["Sst0_%d" % hd], writes=["Sbf0_%d" % hd])
        for j in range(2):
            S.op("sp", (lambda j: lambda e: e.dma_start(out=Sst[:, 1 + j, :, :], in_=sgla_in[j].rearrange("h k v -> k h v")))(j),
                 writes=["Sst%d_%d" % (1 + j, i) for i in range(4)], dma="sgla%d" % j)
            for hd in range(4):
                S.op("act", (lambda j, hd: lambda e: e.activation(out=Sbf[:, 1 + j, hd, :], in_=Sst[:, 1 + j, hd, :], func=AF.Copy))(j, hd),
                     reads=["Sst%d_%d" % (1 + j, hd)], writes=["Sbf%d_%d" % (1 + j, hd)])
        S.fence()
        l0_mixer(x_main, x_mprev, band0m_in, NT, True, True)
        wts = ffn(0, NT, lambda: ple_pre(0))
        win_ = ple(0, p_main[0], p_smp[0], NT, wts, gla_pre)
        gla(NT, True, win_)
        S.op("sp", lambda e: e.dma_start(out=gs_main.rearrange("h k v -> k h v"), in_=Sst[:, 0, :, :]),
             reads=["Sst0_%d" % i for i in range(4)], dma="o_gsm")
        for j in range(2):
            S.op("sp", (lambda j: lambda e: e.dma_start(out=gs_smp[j].rearrange("h k v -> k h v"), in_=Sst[:, 1 + j, :, :]))(j),
                 reads=["Sst%d_%d" % (1 + j, i) for i in range(4)], dma="o_gss%d" % j)
        wts = ffn(1, NT, lambda: ple_pre(1))
        ple(1, p_main[1], p_smp[1], NT, wts)
        final(NT)
        counts = S.emit()
    return nc, counts


def _consts():
    c = {}
    c["c_ident"] = np.eye(128, dtype=np.float32)
    sel = np.zeros((64, 64), np.float32)
    for j in range(2):
        for t in range(16):
            sel[32 * j + 15 + t, 32 * j + t] = 1.0
    c["c_sel"] = sel

    def band_mats(start):
        band = np.zeros((4, 128, 128), np.float32)
        halo = np.zeros((4, 128, 128), np.float32)
        for wi, w in enumerate(WINS):
            for t in range(128):
                cnt = w if start is None else min(w, start + t + 1)
                for i in range(w):
                    s = t - i
                    if s >= 0:
                        band[wi, s, t] += 1.0 / cnt
                    else:
                        halo[wi, 128 + s, t] += 1.0 / cnt
                band[wi, t, t] -= 1.0
        return band, halo

    band, halo = band_mats(None)
    c["c_band"] = band
    c["c_halo"] = halo
    c["c_band0w"] = band_mats(0)[0]
    bs = np.zeros((4, 64, 64), np.float32)
    for wi, w in enumerate(WINS):
        for j in range(2):
            for t in range(16):
                for i in range(w):
                    bs[wi, 32 * j + 15 + t - i, 32 * j + t] += 1.0 / w
                bs[wi, 32 * j + 15 + t, 32 * j + t] -= 1.0
    c["c_bands"] = bs
    m = np.arange(128)[:, None]
    l = np.arange(128)[None, :]
    c["c_tri"] = ((m <= l) * (-1.0 / 16.0)).astype(np.float32)
    c["c_u"] = ((m > l) * (-1.0 / 16.0)).astype(np.float32)
    c["c_cm"] = (m <= l).astype(np.float32)
    m = np.arange(64)[:, None]
    l = np.arange(64)[None, :]
    valid = ((m % 32) < 16) & ((l % 32) < 16) & ((m // 32) == (l // 32))
    c["c_tris"] = ((valid & (m <= l)) * (-1.0 / 16.0)).astype(np.float32)
    c["c_us"] = ((valid & (m > l)) * (-1.0 / 16.0)).astype(np.float32)
    c["c_cms"] = (valid & (m <= l)).astype(np.float32)
    c["band0_first"] = band_mats(0)[0]
    return c


_CACHE = {}


def kernel(**inputs):
    f = lambda a: np.ascontiguousarray(np.asarray(a, dtype=np.float32))
    inp = {k: f(v) for k, v in inputs.items()}
    if "nc" not in _CACHE:
        _CACHE["nc"] = build_program()[0]
    nc = _CACHE["nc"]
    cst = _consts()
    band0_first = cst.pop("band0_first")
    xp, xsm = inp["x_prompt"], inp["x_sample"]
    pp, psm = inp["p_prompt"], inp["p_sample"]
    zeros_tile = np.zeros((128, D), np.float32)
    in_maps = []
    for c in range(8):
        b, half = c // 2, c % 2
        m = dict(cst)
        m["x_main"] = f(xp[b, half * SEG:(half + 1) * SEG])
        m["x_mprev"] = f(xp[b, SEG - 128:SEG]) if half == 1 else zeros_tile
        m["x_warm"] = f(xp[b, 0:SEG])
        m["x_wprev"] = zeros_tile
        m["xs"] = f(xsm[2 * c:2 * c + 2].reshape(32, D))
        m["hist"] = f(inp["state_pool"][0, 2 * c:2 * c + 2].reshape(30, D))
        m["sgla"] = f(inp["state_gla"][0, 2 * c:2 * c + 2])
        m["p_main"] = f(pp[:, b, half * SEG:(half + 1) * SEG])
        m["p_warm"] = f(pp[0, b, 0:SEG])
        m["p_smp"] = f(psm[:, 2 * c:2 * c + 2].reshape(2, 32, 256))
        m["useprev"] = np.full((128, 1), float(half), np.float32)
        m["band0m"] = band0_first if half == 0 else cst["c_band"]
        m["norm_mix"] = inp["norm_mix"]; m["norm_ffn"] = inp["norm_ffn"]; m["norm_ple"] = inp["norm_ple"]
        m["norm_final"] = inp["norm_final"]
        m["w_pool"] = f(inp["w_pool"][0]); m["b_pool"] = f(inp["b_pool"][0]); m["pool_scale"] = f(inp["pool_scale"][0])
        m["w_gla_in"] = f(inp["w_gla_in"][0]); m["w_gla_gate_up"] = f(inp["w_gla_gate_up"][0]); m["b_gla_gate"] = f(inp["b_gla_gate"][0])
        m["gla_norm"] = f(inp["gla_norm"][0]); m["w_gla_out"] = f(inp["w_gla_out"][0])
        m["w_ffn_gate"] = inp["w_ffn_gate"]; m["w_ffn_up"] = inp["w_ffn_up"]; m["w_ffn_down"] = inp["w_ffn_down"]
        m["w_ple_proj"] = inp["w_ple_proj"]; m["w_ple_gate"] = inp["w_ple_gate"]
        in_maps.append(m)
    res = run_bass_kernel_spmd(nc, in_maps, core_ids=list(range(8)))
    R = res.results
    y_prompt = np.stack([np.concatenate([R[2 * b]["y_main"], R[2 * b + 1]["y_main"]], 0) for b in range(4)], 0)
    y_sample = np.concatenate([R[c]["y_smp"].reshape(2, 16, D) for c in range(8)], 0)
    pool_p = np.stack([R[2 * b + 1]["ps_main"] for b in range(4)], 0)[None]
    pool_s = np.concatenate([R[c]["ps_smp"] for c in range(8)], 0)[None]
    gla_p = np.stack([R[2 * b + 1]["gs_main"] for b in range(4)], 0)[None]
    gla_s = np.concatenate([R[c]["gs_smp"] for c in range(8)], 0)[None]
    return tuple(np.ascontiguousarray(a.astype(np.float32)) for a in (y_prompt, y_sample, pool_p, pool_s, gla_p, gla_s))
```
